# Optimizing a Trainium2 kernel written in Bass

```python
import jax, jax.numpy as jnp
from jax import lax
import numpy as np

D_MODEL = 1024
BATCH = 16
SEQ = 2048
DEPTH = 1

GRID_W = 64
CTX_LEN = 256
EPS = 1e-6
CONV_WIDTH = D_MODEL
CONV_K = 31
GLA_HEADS = 4
GLA_DK = D_MODEL // 2
GLA_DV = D_MODEL
HEAD_K = GLA_DK // GLA_HEADS
HEAD_V = GLA_DV // GLA_HEADS
GATE_RANK = 16
GATE_TAU = 16.0
CHUNK = 64
IN_SIZES = (CONV_WIDTH, CONV_WIDTH, CONV_WIDTH, GLA_DK, GLA_DK, GLA_DV, GATE_RANK, GATE_RANK, GLA_DV, D_MODEL, D_MODEL)
N_IN = 3 * CONV_WIDTH + 2 * GLA_DK + 2 * GLA_DV + 2 * GATE_RANK + 2 * D_MODEL
STATE_SIZES = (GLA_DK, GLA_DV, GATE_RANK, GATE_RANK)
STATE_LO = 3 * CONV_WIDTH + GLA_DK
STATE_HI = STATE_LO + GLA_DK + GLA_DV + 2 * GATE_RANK

kernel_name = "hybrid_conformer_gla_dit_block"


def split_cols(p, sizes):
    idx = [int(i) for i in np.cumsum(sizes)[:-1]]
    return jnp.split(p, idx, axis=-1)


def rmsnorm(x, g):
    xf = x.astype(jnp.float32)
    y = xf * lax.rsqrt(jnp.mean(xf * xf, axis=-1, keepdims=True) + EPS)
    return y * g.astype(jnp.float32)


def layernorm(x, g, b):
    xf = x.astype(jnp.float32)
    mu = jnp.mean(xf, axis=-1, keepdims=True)
    var = jnp.mean(jnp.square(xf - mu), axis=-1, keepdims=True)
    return (xf - mu) * lax.rsqrt(var + EPS) * g + b


def ada_mod(cvec, w, b):
    m = jax.nn.silu(cvec) @ w + b
    return jnp.split(m, 3, axis=-1)


def dwconv(x, w, b):
    C = x.shape[-1]
    y = lax.conv_general_dilated(x, w.astype(x.dtype)[:, None, :], (1,), [(CONV_K // 2, CONV_K // 2)],
                                 dimension_numbers=('NWC', 'WIO', 'NWC'), feature_group_count=C)
    return y + b.astype(x.dtype)


def axial_dwconv(a, w, b, rows):
    Bn, S, C = a.shape
    half = C // 2
    ah = a[..., :half].reshape(Bn * rows, GRID_W, half)
    yh = dwconv(ah, w[:, :half], b[:half]).reshape(Bn, S, half)
    av = a[..., half:].reshape(Bn, rows, GRID_W, C - half).transpose(0, 2, 1, 3).reshape(Bn * GRID_W, rows, C - half)
    yv = dwconv(av, w[:, half:], b[half:]).reshape(Bn, GRID_W, rows, C - half).transpose(0, 2, 1, 3).reshape(Bn, S, C - half)
    return jnp.concatenate([yh, yv], axis=-1)


def conv_branch(glu_v, glu_g, z, conv_fn, ln_g, ln_b, proj):
    a = glu_v * jax.nn.sigmoid(glu_g)
    a = conv_fn(a)
    a = jax.nn.silu(layernorm(a, ln_g, ln_b))
    return (a * jax.nn.silu(z)) @ proj


def heads(t, hd):
    Bn, T, _ = t.shape
    return t.reshape(Bn, T, GLA_HEADS, hd).transpose(0, 2, 1, 3).astype(jnp.float32)


def log_decay(lr, up, bias):
    return heads(jax.nn.log_sigmoid((lr @ up + bias).astype(jnp.float32)) / GATE_TAU, HEAD_K)


def flip(t):
    return jnp.flip(t, axis=2)


def gla_final_state(k, v, g):
    b = jnp.cumsum(g, axis=2)
    w = jnp.exp(b[:, :, -1:, :] - b)
    return jnp.einsum('bhtd,bhte->bhde', k * w, v)


def gla_chunk_scan(q, k, v, g, s0):
    Bn, H, T, _ = q.shape
    dv = v.shape[-1]
    n = T // CHUNK

    def chunks(t):
        return jnp.moveaxis(t.reshape(Bn, H, n, CHUNK, t.shape[-1]), 2, 0)

    lower = jnp.tril(jnp.ones((CHUNK, CHUNK), dtype=bool))

    def step(S, inp):
        qc, kc, vc, gc = inp
        b = jnp.cumsum(gc, axis=2)
        o_inter = jnp.einsum('bhld,bhde->bhle', qc * jnp.exp(b), S)
        rel = jnp.where(lower[:, :, None], b[:, :, :, None, :] - b[:, :, None, :, :], -jnp.inf)
        A = jnp.einsum('bhid,bhjd,bhijd->bhij', qc, kc, jnp.exp(rel))
        o_intra = jnp.einsum('bhij,bhje->bhie', A, vc)
        b_last = b[:, :, -1:, :]
        S_new = jnp.exp(b_last[:, :, 0, :, None]) * S + jnp.einsum('bhld,bhle->bhde', kc * jnp.exp(b_last - b), vc)
        return S_new, o_inter + o_intra

    S_fin, o = lax.scan(step, s0, (chunks(q), chunks(k), chunks(v), chunks(g)))
    return jnp.moveaxis(o, 0, 2).reshape(Bn, H, T, dv), S_fin


def bidir_gla(q, k, v, gf, gb, s_f, s_b):
    o_f, _ = gla_chunk_scan(q, k, v, gf, s_f)
    o_b, _ = gla_chunk_scan(flip(q), flip(k), flip(v), flip(gb), s_b)
    return o_f + flip(o_b)


def gla_output(o, r, norm_g, proj):
    o = o * lax.rsqrt(jnp.mean(o * o, axis=-1, keepdims=True) + EPS) * norm_g
    Bn, H, T, dv = o.shape
    o = o.transpose(0, 2, 1, 3).reshape(Bn, T, H * dv)
    return (o * jax.nn.silu(r)) @ proj


def setup_inputs(seed: int = 0) -> dict:
    key = jax.random.key(seed)
    ks = jax.random.split(key, 24)

    def nrm(k, shape, scale=1.0):
        return jax.random.normal(k, shape, jnp.float32) * scale

    return {
        "x": nrm(ks[0], (BATCH, SEQ, D_MODEL)),
        "c": nrm(ks[1], (BATCH, D_MODEL)),
        "ctx": nrm(ks[2], (BATCH, CTX_LEN, D_MODEL)),
        "c_ctx": nrm(ks[3], (D_MODEL,)),
        "ada_w": nrm(ks[4], (DEPTH, D_MODEL, 3 * D_MODEL), D_MODEL ** -0.5),
        "ada_b": nrm(ks[5], (DEPTH, 3 * D_MODEL), 0.02),
        "norm_g": 1.0 + nrm(ks[6], (DEPTH, D_MODEL), 0.02),
        "w_in": nrm(ks[7], (DEPTH, D_MODEL, N_IN), D_MODEL ** -0.5),
        "b_in": nrm(ks[8], (DEPTH, N_IN), 0.02),
        "conv_w": nrm(ks[9], (DEPTH, CONV_K, CONV_WIDTH), CONV_K ** -0.5),
        "conv_b": nrm(ks[10], (DEPTH, CONV_WIDTH), 0.02),
        "conv_ln_g": 1.0 + nrm(ks[11], (DEPTH, CONV_WIDTH), 0.02),
        "conv_ln_b": nrm(ks[12], (DEPTH, CONV_WIDTH), 0.02),
        "conv_proj": nrm(ks[13], (DEPTH, CONV_WIDTH, D_MODEL), CONV_WIDTH ** -0.5),
        "decay_up_fwd": nrm(ks[14], (DEPTH, GATE_RANK, GLA_DK), GATE_RANK ** -0.5),
        "decay_bias_fwd": nrm(ks[15], (DEPTH, GLA_DK), 0.1),
        "decay_up_bwd": nrm(ks[16], (DEPTH, GATE_RANK, GLA_DK), GATE_RANK ** -0.5),
        "decay_bias_bwd": nrm(ks[17], (DEPTH, GLA_DK), 0.1),
        "gla_norm_g": 1.0 + nrm(ks[18], (DEPTH, HEAD_V), 0.02),
        "gla_proj": nrm(ks[19], (DEPTH, GLA_DV, D_MODEL), GLA_DV ** -0.5),
        "w_out": nrm(ks[20], (DEPTH, D_MODEL, D_MODEL), D_MODEL ** -0.5),
        "final_norm_g": 1.0 + nrm(ks[21], (D_MODEL,), 0.02),
    }


def reference(x, c, ctx, c_ctx, ada_w, ada_b, norm_g, w_in, b_in, conv_w, conv_b, conv_ln_g, conv_ln_b,
              conv_proj, decay_up_fwd, decay_bias_fwd, decay_up_bwd, decay_bias_bwd, gla_norm_g, gla_proj,
              w_out, final_norm_g):
    Bn, S, _ = x.shape
    rows = S // GRID_W
    h = x
    hc = ctx
    for l in range(DEPTH):
        last = l == DEPTH - 1
        shift, scale, gate = ada_mod(c, ada_w[l], ada_b[l])
        shift_c, scale_c, gate_c = ada_mod(c_ctx, ada_w[l], ada_b[l])
        u = rmsnorm(h, norm_g[l]) * (1.0 + scale[:, None, :]) + shift[:, None, :]
        uc = rmsnorm(hc, norm_g[l]) * (1.0 + scale_c) + shift_c

        if last:
            pc = uc @ w_in[l][:, STATE_LO:STATE_HI] + b_in[l][STATE_LO:STATE_HI]
            kc_, vc_, afc, abc = split_cols(pc, STATE_SIZES)
        else:
            (gvc, ggc, zc, qc_, kc_, vc_, afc, abc, rc, mgc_conv, mgc_gla) = split_cols(uc @ w_in[l] + b_in[l], IN_SIZES)
        k_ctx = heads(kc_, HEAD_K)
        v_ctx = heads(vc_, HEAD_V)
        gf_ctx = log_decay(afc, decay_up_fwd[l], decay_bias_fwd[l])
        gb_ctx = log_decay(abc, decay_up_bwd[l], decay_bias_bwd[l])
        s_f = gla_final_state(k_ctx, v_ctx, gf_ctx)
        s_b = gla_final_state(flip(k_ctx), flip(v_ctx), flip(gb_ctx))

        (gv, gg, z, q_, k_, v_, af, ab, r, mg_conv, mg_gla) = split_cols(u @ w_in[l] + b_in[l], IN_SIZES)
        y_conv = conv_branch(gv, gg, z, lambda a: axial_dwconv(a, conv_w[l], conv_b[l], rows),
                             conv_ln_g[l], conv_ln_b[l], conv_proj[l])
        q = heads(q_, HEAD_K) * (HEAD_K ** -0.5)
        k = heads(k_, HEAD_K)
        v = heads(v_, HEAD_V)
        gf = log_decay(af, decay_up_fwd[l], decay_bias_fwd[l])
        gb = log_decay(ab, decay_up_bwd[l], decay_bias_bwd[l])
        o = bidir_gla(q, k, v, gf, gb, s_f, s_b)
        y_gla = gla_output(o, r, gla_norm_g[l], gla_proj[l])
        merged = jax.nn.sigmoid(mg_conv) * y_conv + jax.nn.sigmoid(mg_gla) * y_gla
        h_new = h + gate[:, None, :] * (merged @ w_out[l])

        if not last:
            yc_conv = conv_branch(gvc, ggc, zc, lambda a: dwconv(a, conv_w[l], conv_b[l]),
                                  conv_ln_g[l], conv_ln_b[l], conv_proj[l])
            q_ctx = heads(qc_, HEAD_K) * (HEAD_K ** -0.5)
            zero_state = jnp.zeros((Bn, GLA_HEADS, HEAD_K, HEAD_V), jnp.float32)
            oc = bidir_gla(q_ctx, k_ctx, v_ctx, gf_ctx, gb_ctx, zero_state, zero_state)
            yc_gla = gla_output(oc, rc, gla_norm_g[l], gla_proj[l])
            merged_c = jax.nn.sigmoid(mgc_conv) * yc_conv + jax.nn.sigmoid(mgc_gla) * yc_gla
            hc = hc + gate_c * (merged_c @ w_out[l])
        h = h_new
    return rmsnorm(h, final_norm_g)
```

```python
import numpy as np
import ml_dtypes
from contextlib import ExitStack
import concourse.bass as bass
import concourse.mybir as mybir
from concourse.bass_utils import run_bass_kernel_spmd

F32 = mybir.dt.float32
F32R = mybir.dt.float32r
BF16 = mybir.dt.bfloat16
AF = mybir.ActivationFunctionType
ALU = mybir.AluOpType

D = 1024
T = 2048
TC = 256
TT = T + TC
NT = TT // 128
EPS = 1e-6
GV, GG, Z, Q, K, V, AFO, ABO, R, MGC, MGG = 0, 1024, 2048, 3072, 3584, 4096, 5120, 5136, 5152, 6176, 7200
NIN = 8224
QSCALE = 128 ** -0.5

VC_C = 0
VC_ADAB = 24
VC_NG = 40
VC_BGV, VC_BGG, VC_BZ = 48, 56, 64
VC_BQ, VC_BK = 72, 76
VC_BR, VC_BMGC, VC_BMGG = 80, 88, 96
VC_CB, VC_LNG, VC_LNB = 104, 112, 120
VC_GNG = 128
VC_BAF, VC_BAB = 130, 131
VC_CW = 132
NV = 388


class DSem:
    def __init__(self, sem):
        self.sem = sem
        self.count = 0


class Buf:
    __slots__ = ("name", "last_w", "readers")

    def __init__(self, name=""):
        self.name = name
        self.last_w = None
        self.readers = []


class EngState:
    def __init__(self, name, sem):
        self.name = name
        self.sem = sem
        self.count = 0
        self.seen = {}
        self.prog = []


class Sched:
    def __init__(self, nc, stack):
        self.nc = nc
        self.stack = stack
        self.engs = {}
        for n in ["tensor", "vector", "scalar", "gpsimd", "sync"]:
            sem = stack.enter_context(nc.semaphore("s_" + n))
            self.engs[n] = EngState(n, sem)
        self.dsems = []
        self.ninst = 0

    def new_dsem(self):
        d = DSem(self.stack.enter_context(self.nc.semaphore("d%d" % len(self.dsems))))
        self.dsems.append(d)
        return d

    def _waits(self, eng, reads, writes, dsem=None):
        deps = []
        for b in reads:
            if b.last_w is not None:
                deps.append(b.last_w)
        for b in writes:
            if b.last_w is not None:
                if not (dsem is not None and b.last_w[0] is dsem.sem):
                    deps.append(b.last_w)
            deps.extend(b.readers)
        best = {}
        for (sem, val) in deps:
            if sem is eng.sem and eng.name == "tensor":
                continue
            k = id(sem)
            if eng.seen.get(k, 0) >= val:
                continue
            if k not in best or best[k][1] < val:
                best[k] = (sem, val)
        for k, (sem, val) in best.items():
            eng.seen[k] = val
        return list(best.values())

    def emit(self, engname, fn, reads=(), writes=(), signal=True):
        eng = self.engs[engname]
        waits = self._waits(eng, reads, writes)
        if signal:
            eng.count += 1
            tok = (eng.sem, eng.count)
        else:
            assert engname == "tensor"
            tok = (eng.sem, eng.count + 1)
        sem = eng.sem

        def run(e, waits=waits, fn=fn, signal=signal, sem=sem):
            for (s, v) in waits:
                e.wait_ge(s, v)
            inst = fn(e)
            if signal:
                inst.then_inc(sem, 1)

        eng.prog.append(run)
        self.ninst += 1
        for b in writes:
            b.last_w = tok
            b.readers = []
        for b in reads:
            if b not in writes:
                b.readers.append(tok)
        return tok

    def dma(self, qname, fn, dsem, reads=(), writes=()):
        eng = self.engs[qname]
        waits = self._waits(eng, reads, writes, dsem=dsem)
        dsem.count += 16
        tok = (dsem.sem, dsem.count)

        def run(e, waits=waits, fn=fn, s=dsem.sem):
            for (ws, v) in waits:
                e.wait_ge(ws, v)
            fn(e).then_inc(s, 16)

        eng.prog.append(run)
        self.ninst += 1
        for b in writes:
            b.last_w = tok
            b.readers = []
        for b in reads:
            if b not in writes:
                b.readers.append(tok)
        return tok

    def barrier(self):
        toks = [(e.sem, e.count) for e in self.engs.values() if e.count > 0]
        toks += [(d.sem, d.count) for d in self.dsems if d.count > 0]
        for eng in self.engs.values():
            w = []
            for (s, v) in toks:
                if s is eng.sem and eng.name == "tensor":
                    continue
                if eng.seen.get(id(s), 0) < v:
                    eng.seen[id(s)] = v
                    w.append((s, v))
            if w:
                def run(e, w=w):
                    for (s, v) in w:
                        e.wait_ge(s, v)
                eng.prog.append(run)

    def wait_all(self, engname, toks):
        eng = self.engs[engname]

        def run(e, toks=list(toks)):
            for (s, v) in toks:
                e.wait_ge(s, v)

        eng.prog.append(run)

    def build(self):
        nc = self.nc
        with nc.Block() as block:
            for n, dec in [("tensor", block.tensor), ("vector", block.vector), ("scalar", block.scalar),
                           ("gpsimd", block.gpsimd), ("sync", block.sync)]:
                prog = self.engs[n].prog

                def body(e, prog=prog):
                    for r in prog:
                        r(e)

                dec(body)


class Prog:
    def __init__(self, nb=2, dbg=None, stop=None, nodump=False):
        self.nodump = nodump
        self.nb = nb
        self.dbg = dbg
        self.stop = stop
        self.layout = {}
        self.nc = bass.Bass("TRN2", target_bir_lowering=False, dynamic_dma_scratch_size=4096)

    def mm(self, out, lhsT, rhs, start, stop, reads, writes, sig=False):
        self.S.emit("tensor", lambda e: e.matmul(out, lhsT=lhsT, rhs=rhs, start=start, stop=stop),
                    reads=reads, writes=writes, signal=(stop or sig))

    def tr(self, out, in_, reads, writes, signal=True):
        ident = self.ident_bf
        self.S.emit("tensor", lambda e: e.transpose(out, in_, ident), reads=reads, writes=writes, signal=signal)

    def act(self, out, in_, func, reads, writes, bias=None, scale=None, accum=None):
        kw = {}
        if bias is not None:
            kw["bias"] = bias
        if scale is not None:
            kw["scale"] = scale
        if accum is not None:
            kw["accum_out"] = accum
        self.S.emit("scalar", lambda e: e.activation(out=out, in_=in_, func=func, **kw), reads=reads, writes=writes)

    def tt(self, out, in0, in1, op, reads, writes, eng="vector"):
        self.S.emit(eng, lambda e: e.tensor_tensor(out=out, in0=in0, in1=in1, op=op), reads=reads, writes=writes)

    def stt(self, out, in0, scalar, in1, op0, op1, reads, writes):
        self.S.emit("vector", lambda e: e.scalar_tensor_tensor(out=out, in0=in0, scalar=scalar, in1=in1, op0=op0, op1=op1),
                    reads=reads, writes=writes)

    def ts(self, out, in0, s1, s2, op0, op1, reads, writes, eng="vector"):
        self.S.emit(eng, lambda e: e.tensor_scalar(out=out, in0=in0, scalar1=s1, scalar2=s2, op0=op0, op1=op1),
                    reads=reads, writes=writes)

    def cp(self, out, in_, reads, writes, eng="vector"):
        if eng == "scalar":
            self.S.emit(eng, lambda e: e.copy(out=out, in_=in_), reads=reads, writes=writes)
        else:
            self.S.emit(eng, lambda e: e.tensor_copy(out=out, in_=in_), reads=reads, writes=writes)

    def memset(self, ap, val, writes, eng="gpsimd"):
        self.S.emit(eng, lambda e: e.memset(ap, val), reads=(), writes=writes)

    def recip(self, out, in_, reads, writes):
        self.S.emit("vector", lambda e: e.reciprocal(out=out, in_=in_), reads=reads, writes=writes)

    def view(self, off, dtype, shape, parts=128, name=None):
        if name is not None:
            self.layout[name] = (off, "f32" if dtype in (F32, F32R) else "bf16", tuple(shape), parts)
        esz = 4 if dtype in (F32, F32R) else 2
        n = int(np.prod(shape))
        nbytes = n * esz
        assert off % 4 == 0 and nbytes % 4 == 0
        assert off + nbytes <= self.arena_bytes, (off, nbytes, self.arena_bytes)
        ap = self.arena[0:parts, off // 4:(off + nbytes) // 4]
        if dtype != F32:
            ap = ap.bitcast(dtype)
        if len(shape) == 2:
            ap = ap.rearrange("p (a b) -> p a b", a=shape[0])
        elif len(shape) == 3:
            ap = ap.rearrange("p (a b c) -> p a b c", a=shape[0], b=shape[1])
        return ap

    def wload(self, pieces):
        i = self.wslot_i
        self.wslot_i = (i + 1) % len(self.wslots)
        view, buf, dsem = self.wslots[i]
        c = 0
        for (src, col0, ncols) in pieces:
            srcv = src[:, col0:col0 + ncols].rearrange("(kb p) n -> p kb n", p=128)
            dst = view[:, :, c:c + ncols]
            self.S.dma("gpsimd", lambda e, dst=dst, srcv=srcv: e.dma_start(out=dst, in_=srcv), dsem, writes=[buf])
            c += ncols
        assert c <= 512
        return view, buf

    @staticmethod
    def inherit(new_bufs, old_bufs):
        for n in new_bufs:
            for o in old_bufs:
                n.readers.extend(o.readers)
                if o.last_w is not None:
                    n.readers.append(o.last_w)

    def prefetch(self, key, pieces):
        self.pref[key] = self.wload(pieces)

    def wget(self, key, pieces):
        if key in self.pref:
            return self.pref.pop(key)
        return self.wload(pieces)

    def bank(self, pin=False):
        i = self.bank_i
        while i in self.pinned:
            i = (i + 1) % 8
        self.bank_i = (i + 1) % 8
        if pin:
            self.pinned.add(i)
        return self.banks[i], self.bank_bufs[i]

    def unpin(self, bk):
        for i in range(8):
            if self.banks[i] is bk:
                self.pinned.discard(i)

    def build(self):
        nc = self.nc
        nb = self.nb
        dt = nc.dram_tensor
        self.xcat = dt("xcat", [nb, TT, D], F32, kind="ExternalInput").ap()
        self.vecs_d = dt("vecs", [128, NV], F32, kind="ExternalInput").ap()
        self.kvb_d = dt("kvb", [128, 1536], F32, kind="ExternalInput").ap()
        self.adag_d = dt("adag", [128, 1024], F32, kind="ExternalInput").ap()
        self.fng_d = dt("fng", [128, 1024], F32, kind="ExternalInput").ap()
        self.consts_d = dt("consts", [128, 7 * 128 + 64], F32, kind="ExternalInput").ap()
        self.upb_d = dt("upb", [2, 17, 512], F32, kind="ExternalInput").ap()
        self.ada_w = dt("ada_w", [D, 3 * D], F32, kind="ExternalInput").ap()
        self.w_in = dt("w_in", [D, NIN], F32, kind="ExternalInput").ap()
        self.conv_proj = dt("conv_proj", [D, D], F32, kind="ExternalInput").ap()
        self.gla_proj = dt("gla_proj", [D, D], F32, kind="ExternalInput").ap()
        self.w_out = dt("w_out", [D, D], F32, kind="ExternalInput").ap()
        self.out_d = dt("out", [nb, T, D], F32, kind="ExternalOutput").ap()
        if self.stop:
            self.dbg_d = dt("dbg", [128, (219 * 1024 - 6656) // 4], F32, kind="ExternalOutput").ap()

        with ExitStack() as st:
            self.S = S = Sched(nc, st)
            KB = 1024
            self.arena_bytes = 219 * KB - 6656
            self.arena = st.enter_context(nc.sbuf_tensor("arena", [128, self.arena_bytes // 4], F32))[:, :]
            self.banks = [st.enter_context(nc.psum_tensor("bank%d" % i, [128, 512], F32))[:, :] for i in range(8)]
            self.bank_bufs = [Buf("bank%d" % i) for i in range(8)]
            self.bank_i = 0
            self.pinned = set()

            o = 0

            def alloc(nbytes):
                nonlocal o
                r = o
                o += (nbytes + 31) // 32 * 32
                return r

            self.ident_bf = self.view(alloc(256), BF16, (128,))
            ones_bf = self.view(alloc(256), BF16, (128,))
            self.si_bf = self.view(alloc(128), BF16, (64,))
            ident_f = self.view(alloc(512), F32, (128,))
            mask_f = self.view(alloc(512), F32, (128,))
            mask_b = self.view(alloc(512), F32, (128,))
            rt = lambda name, n: st.enter_context(nc.sbuf_tensor(name, [128, n], F32R))[:, :]
            tri_r = [rt("tri_r%d" % i, 128) for i in range(4)]
            ones_r = rt("ones_r", 128)
            self.ysq_r = [rt("ysq_r%d" % i, 512) for i in range(2)]
            ones_f = self.view(alloc(512), F32, (128,))
            vecs = self.view(alloc(NV * 4), F32, (NV,))
            self.vecs = vecs
            kvb_b = self.view(alloc(1536 * 4), F32, (1536,))
            upb = [self.view(alloc(1024), BF16, (512,), parts=17) for _ in range(2)]
            gate_b = [self.view(alloc(4096), F32, (1024,), name='gate_b%d' % i) for i in range(2)]
            gs = self.view(alloc(96), F32, (3, 8), name='gs')
            sh = self.view(alloc(96), F32, (3, 8), name='sh')
            sc_f = self.view(alloc(96), F32, (24,))
            sc_bf = self.view(alloc(64), BF16, (24,))
            small = self.view(alloc(1024), F32, (256,))
            uT = self.view(alloc(8 * TT * 2), BF16, (8, TT), name='uT')
            self.uT = uT
            self.wslots = []
            for i in range(3):
                v = self.view(alloc(8 * 512 * 2), BF16, (8, 512))
                self.wslots.append((v, Buf("w%d" % i), S.new_dsem()))
            self.wslot_i = 0
            self.pref = {}
            DYN = o
            RY = DYN
            R2 = DYN + 64 * KB
            R3 = DYN + 96 * KB
            R3SZ = self.arena_bytes - R3
            assert R3SZ >= 34 * KB, R3SZ

            B_const = Buf("const")
            B_vecs = Buf("vecs")
            d_const = S.new_dsem()
            d_x = [S.new_dsem() for _ in range(3)]
            d_x3 = S.new_dsem()
            d_out = [S.new_dsem() for _ in range(2)]
            d_out2 = S.new_dsem()

            cst_all = self.view(RY, F32, (7 * 128 + 64,))
            cst = cst_all[:, 0:7 * 128].rearrange("p (a b) -> p a b", a=7)
            B_cst = Buf("cst")
            S.dma("sync", lambda e: e.dma_start(out=cst_all, in_=self.consts_d), d_const, writes=[B_cst])
            S.dma("sync", lambda e: e.dma_start(out=vecs, in_=self.vecs_d), d_const, writes=[B_vecs])
            S.dma("sync", lambda e: e.dma_start(out=kvb_b, in_=self.kvb_d), d_const, writes=[B_const])
            adag_b = self.view(RY + 4 * KB, F32, (1024,))
            S.dma("sync", lambda e: e.dma_start(out=adag_b, in_=self.adag_d), d_const, writes=[B_const])
            d_upb = S.new_dsem()
            B_upb = Buf("upb")
            for i in range(2):
                S.dma("gpsimd", lambda e, i=i: e.dma_start(out=upb[i], in_=self.upb_d[i]), d_upb, writes=[B_upb])
            tot = (d_const.sem, d_const.count)
            B_cst.last_w = tot
            B_vecs.last_w = tot
            B_const.last_w = tot
            B_upb.last_w = (d_upb.sem, d_upb.count)
            B_c2 = Buf("c2")
            self.cp(ident_f, cst[:, 0, :], [B_cst], [B_c2])
            self.cp(self.ident_bf, cst[:, 0, :], [B_cst], [B_c2])
            self.cp(self.si_bf, cst_all[:, 7 * 128:7 * 128 + 64], [B_cst], [B_c2])
            self.cp(mask_f, cst[:, 1, :], [B_cst], [B_c2])
            self.cp(mask_b, cst[:, 2, :], [B_cst], [B_c2])
            for i in range(4):
                self.cp(tri_r[i], cst[:, 3 + i, :], [B_cst], [B_c2])
            self.memset(ones_f, 1.0, [B_c2], eng="vector")
            self.memset(ones_bf, 1.0, [B_c2], eng="vector")
            self.cp(ones_r, ones_f, [B_c2], [B_c2])
            CONST = [B_c2, B_vecs, B_const, B_upb]

            self.act(sc_f, vecs[:, VC_C:VC_C + 24], AF.Silu, [B_vecs], [B_c2])
            self.cp(sc_bf, sc_f, [B_c2], [B_c2])
            aw1 = self.view(RY + 8 * KB, BF16, (8, 2048))
            aw2 = self.view(RY + 8 * KB + 32 * KB, BF16, (8, 1024))
            scb = [self.view(RY + 56 * KB + i * 2 * KB, BF16, (8, 128)) for i in range(2)]
            B_awc = [Buf("aw%d" % c) for c in range(6)]
            d_awc = [S.new_dsem() for _ in range(6)]
            awv = self.ada_w.rearrange("(kb p) n -> p kb n", p=128)
            for c in range(4):
                S.dma("gpsimd", lambda e, c=c: e.dma_start(out=aw1[:, :, c * 512:(c + 1) * 512], in_=awv[:, :, c * 512:(c + 1) * 512]), d_awc[c], writes=[B_awc[c]])
            for c in range(2):
                S.dma("gpsimd", lambda e, c=c: e.dma_start(out=aw2[:, :, c * 512:(c + 1) * 512], in_=awv[:, :, 2048 + c * 512:2048 + (c + 1) * 512]), d_awc[4 + c], writes=[B_awc[4 + c]])
            bk, bb = self.bank()
            bk4 = bk[:, 0:64].rearrange("p (n f) -> p n f", f=4)
            for nbk in range(16):
                for kb in range(8):
                    self.mm(bk4[:, nbk, 0:3], aw1[:, kb, nbk * 128:(nbk + 1) * 128], sc_bf[:, kb * 3:(kb + 1) * 3],
                            kb == 0, kb == 7, [B_awc[nbk // 4], B_c2], [bb])
            modv = self.view(R3, F32, (3, 16))
            B_mod = Buf("mod")
            for v in range(3):
                self.tt(modv[:, v, :], bk4[:, :, v], vecs[:, VC_ADAB:VC_ADAB + 16], ALU.add, [bb, B_vecs], [B_mod])
                self.cp(sh[:, v, :], modv[:, v, 0:8], [B_mod], [B_c2])
                self.stt(gs[:, v, :], modv[:, v, 8:16], 1.0, vecs[:, VC_NG:VC_NG + 8], ALU.add, ALU.mult, [B_mod, B_vecs], [B_c2])
            for v in range(nb):
                for kb in range(8):
                    self.ts(scb[v][:, kb, :], ones_bf, sc_f[:, kb * 3 + v:kb * 3 + v + 1], None, ALU.mult, ALU.bypass, [B_c2], [B_c2])
                for half in range(2):
                    bk, bb = self.bank()
                    for kb in range(8):
                        self.mm(bk, scb[v][:, kb, :], aw2[:, kb, half * 512:(half + 1) * 512], kb == 0, kb == 7, [B_awc[4 + half], B_c2], [bb])
                    self.tt(gate_b[v][:, half * 512:(half + 1) * 512], bk, adag_b[:, half * 512:(half + 1) * 512], ALU.add,
                            [bb, B_const], [B_c2])
            S.barrier()

            for bi in range(nb):
                if self.stop == "A":
                    break
                done = self.batch(bi, RY, R2, R3, KB, CONST, dict(
                    ident_f=ident_f, ones_bf=ones_bf, mask_f=mask_f, mask_b=mask_b, tri_r=tri_r, ones_r=ones_r,
                    ones_f=ones_f, kvb_b=kvb_b, upb=upb, d_const=d_const, gate_b=gate_b, gs=gs, sh=sh, small=small,
                    d_x=d_x, d_x3=d_x3, d_out=d_out, d_out2=d_out2))
                if done:
                    break

            if self.stop and not self.nodump:
                S.barrier()
                d_dbg = S.new_dsem()
                nel = self.arena_bytes // 4
                step = nel // 4
                for i in range(4):
                    S.dma("sync", lambda e, i=i: e.dma_start(out=self.dbg_d[:, i * step:(i + 1) * step], in_=self.arena[:, i * step:(i + 1) * step]), d_dbg)
                S.wait_all("sync", [(d_dbg.sem, d_dbg.count)])
            S.wait_all("sync", [(d.sem, d.count) for d in d_out + [d_out2] if d.count > 0])
            S.build()
        return nc

    def batch(self, bi, RY, R2, R3, KB, CONST, C):
        S = self.S
        vecs = self.vecs
        uT = self.uT
        gs, sh, small = C["gs"], C["sh"], C["small"]
        B_c2 = CONST[0]
        w_in = self.w_in

        def vcol(c):
            return vecs[:, c:c + 1]

        self.prefetch(("pair", 0), [(w_in, GV, 256), (w_in, GG, 256)])
        xs = [self.view(RY + i * 4 * KB, F32, (1024,)) for i in range(3)]
        xn = [self.view(RY + 12 * KB + i * 2 * KB, BF16, (1024,)) for i in range(2)]
        junk = self.view(RY + 16 * KB, BF16, (1024,))
        B_xs = [Buf("xs%d" % i) for i in range(3)]
        B_xn = [Buf("xn%d" % i) for i in range(2)]
        B_uT = [Buf("uT%d" % i) for i in range(NT)]
        B_uTo = [Buf("uTo%d" % i) for i in range(NT)]
        B_junk = Buf("junk")
        B_smL = [Buf("small%d" % i) for i in range(NT)]
        xs.append(self.view(RY + 18 * KB, F32, (1024,)))
        B_xs.append(Buf("xs3"))
        if getattr(self, "prev", None):
            self.inherit(B_xs + B_xn + [B_junk], self.prev["c4"])
            self.inherit(B_uT + B_uTo, self.prev["uT"])
        d_x4 = C["d_x"] + [C["d_x3"]]
        pend = {}

        def stage1(tt):
            B_sm = B_smL[tt]
            x_t, bx = xs[tt % 4], B_xs[tt % 4]
            S.dma("sync", lambda e, x_t=x_t, tt=tt: e.dma_start(out=x_t, in_=self.xcat[bi, tt * 128:(tt + 1) * 128, :]),
                  d_x4[tt % 4], writes=[bx])
            ss = small[:, tt:tt + 1]
            ln_ = small[:, 32 + tt:33 + tt]
            rstd = small[:, 64 + tt:65 + tt]
            self.act(junk, x_t, AF.Square, [bx], [B_sm, B_junk], accum=ss)
            self.act(ln_, ss, AF.Ln, [B_sm], [B_sm], bias=EPS, scale=1.0 / D)
            self.act(rstd, ln_, AF.Exp, [B_sm], [B_sm], scale=-0.5)

        def stage2(tt):
            B_sm = B_smL[tt]
            x_t, bx = xs[tt % 4], B_xs[tt % 4]
            rstd = small[:, 64 + tt:65 + tt]
            xn_t, bxn = xn[tt % 2], B_xn[tt % 2]
            self.ts(xn_t, x_t, rstd, None, ALU.mult, ALU.bypass, [bx, B_sm], [bxn])
            bk, bb = self.bank(pin=True)
            bkb = bk.bitcast(BF16)
            bk2, bb2 = self.bank(pin=True)
            bkb2 = bk2.bitcast(BF16)
            for kb in range(6):
                self.tr(bkb[:, kb * 128:(kb + 1) * 128], xn_t[:, kb * 128:(kb + 1) * 128], [bxn, B_c2], [bb], signal=(kb == 5))
            for kb in range(6, 8):
                self.tr(bkb2[:, (kb - 6) * 128:(kb - 5) * 128], xn_t[:, kb * 128:(kb + 1) * 128], [bxn, B_c2], [bb2], signal=(kb == 7))
            pend[tt] = (bk, bkb, bb, bk2, bkb2, bb2)

        def stage3(tt):
            v = 2 if tt < 2 else bi
            bk, bkb, bb, bk2, bkb2, bb2 = pend.pop(tt)
            for kb in range(8):
                dst = uT[:, kb, tt * 128:(tt + 1) * 128]
                if kb < 6:
                    src = bkb[:, kb * 128:(kb + 1) * 128]
                    self.ts(dst, src, gs[:, v, kb:kb + 1], sh[:, v, kb:kb + 1], ALU.mult, ALU.add, [bb, B_c2],
                            [B_uT[tt]] if kb == 5 else [Buf("uTtmp")])
                else:
                    src = bkb2[:, (kb - 6) * 128:(kb - 5) * 128]
                    self.act(dst, src, AF.Identity, [bb2, B_c2], [B_uTo[tt]] if kb == 7 else [Buf("uTtmp")],
                             bias=sh[:, v, kb:kb + 1], scale=gs[:, v, kb:kb + 1])
            self.unpin(bk)
            self.unpin(bk2)

        for tt in range(NT + 2):
            if tt < NT:
                stage1(tt)
            if 0 <= tt - 1 < NT:
                stage2(tt - 1)
            if 0 <= tt - 2 < NT:
                stage3(tt - 2)
        PHB_BUFS = B_xs + B_xn + [B_junk]

        LAT = [B_uT[2 + 4 * i:6 + 4 * i] + B_uTo[2 + 4 * i:6 + 4 * i] for i in range(4)]

        def proj_fm(bk, bb, wv, wb, c0, ncols, tok0, ntok, rbufs):
            for kb in range(8):
                self.mm(bk[0:ncols, 0:ntok], wv[:, kb, c0:c0 + ncols], uT[:, kb, tok0:tok0 + ntok], kb == 0, kb == 7,
                        [wb] + rbufs, [bb])

        if self.stop == "B":
            return True
        y = self.view(RY, F32, (8, T), name='y')
        B_y = [[Buf("y%d_%d" % (cb, t)) for t in range(4)] for cb in range(8)]
        self.inherit([b for row in B_y for b in row], PHB_BUFS)
        LA = 3968
        a_pad = [self.view(R2 + i * 7936, BF16, (LA,), name='a_pad%d' % i) for i in range(2)]
        AS = [[self.view(R2 + 15872 + (i * 2 + q) * 7936, BF16, (LA,)) for q in range(2)] for i in range(2)]
        dg2 = [self.view(R2 + 47616 + i * 4096, BF16, (2, 16, 64)) for i in range(2)]
        sg = [self.view(R2 + 55808 + i * 2 * KB, F32, (512,)) for i in range(2)]
        B_ap = [Buf("apad%d" % i) for i in range(2)]
        B_AS = [Buf("AS%d" % i) for i in range(2)]
        B_dg = [Buf("dg%d" % i) for i in range(2)]
        B_sg = [Buf("sg%d" % i) for i in range(2)]
        if getattr(self, "prev", None):
            self.inherit(B_ap + B_AS + B_dg + B_sg, self.prev["m1"] + self.prev["c3"])
        d_as = [S.new_dsem() for _ in range(2)] if not hasattr(self, "d_as") else self.d_as
        self.d_as = d_as
        wcur = {}
        if bi == 0:
            for i in range(2):
                for q in range(2):
                    self.memset(AS[i][q], 0.0, [B_AS[i]])

        def load_pair(p):
            wcur[p] = self.wget(("pair", p), [(w_in, GV + p * 256, 256), (w_in, GG + p * 256, 256)])

        def conv_proj_stage(cb):
            wv, wb = wcur[cb // 2]
            b_ = cb % 2
            ap_t, bap = a_pad[b_], B_ap[b_]
            if cb in (0, 1, 4, 5):
                self.memset(ap_t, 0.0, [bap])
            for q in range(2):
                for m in range(16):
                    self.ts(dg2[b_][:, q, m, :], self.si_bf, vcol(VC_CW + (cb * 2 + q) * 16 + m), 0.0, ALU.mult, ALU.add,
                            [B_c2, CONST[1]], [B_dg[b_]], eng="gpsimd")
            for t in range(4):
                bkv, bbv = self.bank()
                proj_fm(bkv, bbv, wv, wb, (cb % 2) * 128, 128, 256 + t * 512, 512, LAT[t])
                bkg, bbg = self.bank()
                proj_fm(bkg, bbg, wv, wb, 256 + (cb % 2) * 128, 128, 256 + t * 512, 512, LAT[t])
                sgt, bsg = sg[t % 2], B_sg[t % 2]
                self.act(sgt, bkg, AF.Sigmoid, [bbg, CONST[1]], [bsg], bias=vcol(VC_BGG + cb))
                if cb < 4:
                    dst = ap_t[:, 0:32 * 94].rearrange("p (r w) -> p r w", w=94)[:, t * 8:(t + 1) * 8, 15:79]
                    i0 = bkv.rearrange("p (r w) -> p r w", w=64)
                    i1 = sgt.rearrange("p (r w) -> p r w", w=64)
                else:
                    dst = ap_t[:, (15 + t * 8) * 64:(15 + t * 8 + 8) * 64]
                    i0 = bkv
                    i1 = sgt
                self.stt(dst, i0, vcol(VC_BGV + cb), i1, ALU.add, ALU.mult, [bbv, bsg, CONST[1]], [bap])
            delta = 1 if cb < 4 else 64
            for q in range(2):
                for j in range(2):
                    dst = AS[b_][q][j * 64:(j + 1) * 64, 0:LA - j * delta]
                    src = ap_t[q * 64:(q + 1) * 64, j * delta:LA]
                    S.dma("sync", lambda e, dst=dst, src=src: e.dma_start(out=dst, in_=src), d_as[b_], reads=[bap], writes=[B_AS[b_]])

        def conv_stage(cb):
            b_ = cb % 2
            bas, bdg = B_AS[b_], B_dg[b_]
            for t in range(4):
                bk, bb = self.bank()
                groups_ = []
                for m in range(16):
                    s0 = 2 * m - 15
                    if cb < 4:
                        rhs = [AS[b_][q][:, 0:32 * 94].rearrange("p (r w) -> p r w", w=94)[:, t * 8:(t + 1) * 8, 15 + s0:15 + s0 + 64]
                               for q in range(2)]
                        outv = [bk[q * 64:(q + 1) * 64, :].rearrange("p (r w) -> p r w", w=64) for q in range(2)]
                    else:
                        valid = [(2 * m + j <= 30) and not (t * 8 + 7 + s0 + j < 0 or t * 8 + s0 + j > 31) for j in range(2)]
                        if not any(valid):
                            continue
                        r0 = 15 + t * 8 + s0
                        if r0 < 0 or (r0 + 8) * 64 > LA:
                            assert False, (cb, t, m, r0)
                        rhs = [AS[b_][q][:, r0 * 64:(r0 + 8) * 64] for q in range(2)]
                        outv = [bk[q * 64:(q + 1) * 64, :] for q in range(2)]
                    groups_.append((m, rhs, outv))
                for i, (m, rhs, outv) in enumerate(groups_):
                    for q in range(2):
                        self.mm(outv[q], dg2[b_][:, q, m, :], rhs[q], i == 0, i == len(groups_) - 1, [bas, bdg], [bb])
                self.act(y[:, cb, t * 512:(t + 1) * 512], bk, AF.Identity, [bb, CONST[1]], [B_y[cb][t]], bias=vcol(VC_CB + cb))

        load_pair(0)
        conv_proj_stage(0)
        for cb in range(8):
            if cb + 1 < 8:
                if (cb + 1) % 2 == 0:
                    load_pair((cb + 1) // 2)
                conv_proj_stage(cb + 1)
            conv_stage(cb)
        self.prefetch(("z", 0), [(w_in, Z, 512)])
        CONV_BUFS = B_ap + B_AS + B_dg + B_sg

        if self.stop == "C1a":
            return True
        mean_b = self.view(R2, F32, (T,), name='mean_b')
        rstd_b = self.view(R2 + 8 * KB, F32, (T,), name='rstd_b')
        B_mean = [Buf("mean%d" % t) for t in range(4)]
        B_rstd = [Buf("rstd%d" % t) for t in range(4)]
        ysq = self.ysq_r
        B_ysq = [Buf("ysq%d" % i) for i in range(2)]
        msq = self.view(R3 + 4 * KB, F32, (512,))
        var = self.view(R3 + 6 * KB, F32, (512,))
        sd = self.view(R3 + 8 * KB, F32, (512,))
        B_msq, B_var, B_sd = Buf("msq"), Buf("var"), Buf("sd")
        self.inherit(B_mean + B_rstd + [B_msq, B_var, B_sd], CONV_BUFS)
        ones_f, ones_r = C["ones_f"], C["ones_r"]
        for t in range(4):
            bks, bbs = self.bank()
            for cb in range(8):
                self.mm(bks, ones_f, y[:, cb, t * 512:(t + 1) * 512], cb == 0, cb == 7, [B_c2, B_y[cb][t]], [bbs])
            bkq, bbq = self.bank()
            for cb in range(8):
                yq, byq = ysq[cb % 2], B_ysq[cb % 2]
                self.act(yq, y[:, cb, t * 512:(t + 1) * 512], AF.Square, [B_y[cb][t]], [byq])
                self.mm(bkq, ones_r, yq, cb == 0, cb == 7, [B_c2, byq], [bbq], sig=True)
            self.act(mean_b[:, t * 512:(t + 1) * 512], bks, AF.Copy, [bbs], [B_mean[t]], scale=1.0 / D)
            self.act(msq, bks, AF.Square, [bbs], [B_msq], scale=1.0 / D)
            self.stt(var, bkq, 1.0 / D, msq, ALU.mult, ALU.subtract, [bbq, B_msq], [B_var])
            self.act(sd, var, AF.Ln, [B_var], [B_sd], bias=EPS)
            self.act(rstd_b[:, t * 512:(t + 1) * 512], sd, AF.Exp, [B_sd], [B_rstd[t]], scale=-0.5)

        if self.stop == "C1b":
            return True
        t1 = [self.view(R3 + 10 * KB + i * 2 * KB, F32, (512,)) for i in range(2)]
        t2 = [self.view(R3 + 14 * KB + i * 2 * KB, F32, (512,)) for i in range(2)]
        szb = [self.view(R3 + 18 * KB + i * KB, BF16, (512,)) for i in range(2)]
        sb_ = [self.view(R3 + 20 * KB + i * KB, BF16, (512,)) for i in range(2)]
        B_t1 = [Buf("t1_%d" % i) for i in range(2)]
        B_t2 = [Buf("t2_%d" % i) for i in range(2)]
        B_sz = [Buf("sz%d" % i) for i in range(2)]
        B_s = [Buf("s%d" % i) for i in range(2)]
        self.inherit(B_t1 + B_t2 + B_sz + B_s, CONV_BUFS)
        czT = [self.view(RY + cb * 8 * KB, BF16, (T,), name='czT%d' % cb) for cb in range(8)]
        B_cz = [[Buf("cz%d_%d" % (cb, t)) for t in range(4)] for cb in range(8)]
        wz = {}
        wz[0] = self.wget(("z", 0), [(w_in, Z, 512)])
        self.prefetch(("cp", 0), [(self.conv_proj, 0, 512)])
        i = 0
        for cb in range(8):
            if cb == 2:
                wz[1] = self.wload([(w_in, Z + 512, 512)])
            if cb == 5:
                self.prefetch(("mgc", 0), [(w_in, MGC, 512)])
            wv, wb = wz[cb // 4]
            for t in range(4):
                bkz, bbz = self.bank()
                proj_fm(bkz, bbz, wv, wb, (cb % 4) * 128, 128, 256 + t * 512, 512, LAT[t])
                j = i % 2
                i += 1
                self.act(szb[j], bkz, AF.Silu, [bbz, CONST[1]], [B_sz[j]], bias=vcol(VC_BZ + cb))
                ysl = y[:, cb, t * 512:(t + 1) * 512]
                self.tt(t1[j], ysl, mean_b[:, t * 512:(t + 1) * 512], ALU.subtract, [B_y[cb][t], B_mean[t]], [B_t1[j]], eng="gpsimd")
                self.tt(t2[j], t1[j], rstd_b[:, t * 512:(t + 1) * 512], ALU.mult, [B_t1[j], B_rstd[t]], [B_t2[j]])
                self.act(sb_[j], t2[j], AF.Silu, [B_t2[j], CONST[1]], [B_s[j]], bias=vcol(VC_LNB + cb), scale=vcol(VC_LNG + cb))
                self.tt(czT[cb][:, t * 512:(t + 1) * 512], sb_[j], szb[j], ALU.mult, [B_s[j], B_sz[j]],
                        [B_cz[cb][t], B_y[cb][t // 2]])

        if self.stop == "C1c":
            return True
        m1T = self.view(R2, BF16, (8, T), name='m1T')
        B_m1 = [[Buf("m1_%d_%d" % (nbk, t)) for t in range(4)] for nbk in range(8)]
        self.inherit([b for row in B_m1 for b in row], B_mean + B_rstd)
        sgm = [self.view(R3 + i * 2 * KB, F32, (512,)) for i in range(2)]
        B_sgm = [Buf("sgm%d" % i) for i in range(2)]
        self.inherit(B_sgm, CONV_BUFS + [B_msq, B_var, B_sd])
        i = 0
        for half in range(2):
            wcp = self.wget(("cp", half), [(self.conv_proj, half * 512, 512)])
            wmg = self.wget(("mgc", half), [(w_in, MGC + half * 512, 512)])
            for q4 in range(4):
                nbk = half * 4 + q4
                for t in range(4):
                    bkc, bbc = self.bank()
                    for cb in range(8):
                        self.mm(bkc, wcp[0][:, cb, q4 * 128:(q4 + 1) * 128], czT[cb][:, t * 512:(t + 1) * 512], cb == 0, cb == 7,
                                [wcp[1], B_cz[cb][t]], [bbc])
                    bkm, bbm = self.bank()
                    proj_fm(bkm, bbm, wmg[0], wmg[1], q4 * 128, 128, 256 + t * 512, 512, LAT[t])
                    j = i % 2
                    i += 1
                    self.act(sgm[j], bkm, AF.Sigmoid, [bbm, CONST[1]], [B_sgm[j]], bias=vcol(VC_BMGC + nbk))
                    self.tt(m1T[:, nbk, t * 512:(t + 1) * 512], bkc, sgm[j], ALU.mult, [bbc, B_sgm[j]], [B_m1[nbk][t]])
        self.prefetch(("lr", 0), [(w_in, AFO, 32)])
        C1_BUFS = [b for row in B_cz for b in row] + B_sgm + B_t1 + B_t2 + B_sz + B_s + [B_msq, B_var, B_sd]

        if self.stop == "C1d":
            return True
        ogT = self.view(RY, BF16, (8, T), name='ogT')
        B_og = [[Buf("og%d_%d" % (blk, c)) for c in range(16)] for blk in range(8)]
        HB = RY + 32 * KB
        qT = self.view(HB, BF16, (T,))
        kT = self.view(HB + 4 * KB, BF16, (TT,))
        kv_tm = self.view(HB + 4 * KB + 4608, BF16, (NT, 384))
        lrT = [self.view(R3 + i * 4608, BF16, (TT,), parts=17, name='lrT%d' % i) for i in range(2)]
        o_f = self.view(R3 + 9216, F32, (16, 256), name='o_f')
        G0 = R3 + 9216 + 16 * KB
        goff = [G0]

        def galloc(nbytes):
            r = goff[0]
            goff[0] += (nbytes + 31) // 32 * 32
            return r

        e1 = self.view(galloc(2048), F32, (512,))
        sp = self.ysq_r
        ekl = self.view(galloc(1024), BF16, (512,))
        ebT = self.view(galloc(1024), BF16, (512,))
        enbT = self.view(galloc(1024), BF16, (512,))
        kl = [self.view(HB + 22528 + i * 1024, BF16, (512,)) for i in range(2)]
        qeT = [self.view(HB + 22528 + 2048 + i * 1024, BF16, (512,)) for i in range(2)]
        keT = [self.view(HB + 22528 + 4096 + i * 1024, BF16, (512,)) for i in range(2)]
        A_sb = [self.view(galloc(256), BF16, (128,)) for _ in range(2)]
        ebl = [self.view(galloc(32), F32, (4,)) for _ in range(2)]
        S_f = [self.view(galloc(1024), F32, (256,)) for _ in range(2)]
        S_bf = [self.view(galloc(512), BF16, (256,)) for _ in range(2)]
        ong = [self.view(HB + 28672 + i * 2048, BF16, (4, 256)) for i in range(2)]
        onj = self.view(galloc(512), BF16, (256,))
        fin = self.view(galloc(192), F32, (48,))
        assert goff[0] <= self.arena_bytes, (goff[0], self.arena_bytes)
        mk = lambda n: [Buf(n + str(i)) for i in range(2)]
        B_sp, B_kl, B_qeT, B_keT, B_A, B_ebl = (mk(n) for n in ["sp", "kl", "qeT", "keT", "A", "ebl"])
        B_e1, B_ekl, B_ebT, B_enbT = Buf("e1"), Buf("ekl"), Buf("ebT"), Buf("enbT")
        B_Sf = mk("Sf")
        B_Sbf, B_ong = mk("Sbf"), mk("ong")
        B_fin = Buf("fin")
        B_lr = [Buf("lr0"), Buf("lr1")]
        B_of = [Buf("of%d" % c) for c in range(16)]
        B_q = [Buf("q%d" % t) for t in range(4)]
        B_k = [Buf("k%d" % t) for t in range(5)]
        B_kv = [Buf("kv%d" % t) for t in range(NT)]
        self.inherit([b for row in B_og for b in row] + B_lr + B_of + B_q + B_k + B_kv + B_kl + B_qeT + B_keT + B_A + B_ebl + B_Sf + B_Sbf
                     + [B_e1, B_ekl, B_ebT, B_enbT, B_fin], C1_BUFS)
        tri_r, mask_f, mask_b = C["tri_r"], C["mask_f"], C["mask_b"]
        upb, kvb_b = C["upb"], C["kvb_b"]

        wlr = self.wget(("lr", 0), [(w_in, AFO, 32)])
        for d_ in range(2):
            self.memset(lrT[d_], 1.0, [B_lr[d_]], eng="vector")
        toktiles = [(0, 256, B_uT[0:2] + B_uTo[0:2])] + [(256 + t * 512, 512, LAT[t]) for t in range(4)]
        for d_ in range(2):
            for (tok0, ntok, rb) in toktiles:
                bk, bb = self.bank()
                proj_fm(bk, bb, wlr[0], wlr[1], d_ * 16, 16, tok0, ntok, rb)
                self.act(lrT[d_][0:16, tok0:tok0 + ntok], bk[0:16, 0:ntok], AF.Identity, [bb, CONST[1]], [B_lr[d_]],
                         bias=vecs[0:16, VC_BAF + d_:VC_BAF + d_ + 1])

        def load_head(h):
            w1 = self.wload([(w_in, Q + h * 128, 128), (w_in, K + h * 128, 128), (w_in, V + h * 256, 256)])
            return w1

        hw = {}

        def PROJ(h):
            (wv, wb) = hw.pop(h)
            for t in range(4):
                bk, bb = self.bank()
                proj_fm(bk, bb, wv, wb, 0, 128, 256 + t * 512, 512, LAT[t])
                self.ts(qT[:, t * 512:(t + 1) * 512], bk, vcol(VC_BQ + h), QSCALE, ALU.add, ALU.mult, [bb, CONST[1]], [B_q[t]])
            for ti, (tok0, ntok, rb) in enumerate(toktiles):
                bk, bb = self.bank()
                proj_fm(bk, bb, wv, wb, 128, 128, tok0, ntok, rb)
                self.act(kT[:, tok0:tok0 + ntok], bk[:, 0:ntok], AF.Identity, [bb, CONST[1]], [B_k[ti]], bias=vcol(VC_BK + h))
            for tile in range(NT):
                bk, bb = self.bank()
                for kb in range(8):
                    self.mm(bk[:, 0:384], uT[:, kb, tile * 128:(tile + 1) * 128], wv[:, kb, 128:512], kb == 0, kb == 7,
                            [wb, B_uT[tile], B_uTo[tile]], [bb])
                self.tt(kv_tm[:, tile, :], bk[:, 0:384], kvb_b[:, h * 384:(h + 1) * 384], ALU.add, [bb, CONST[2]], [B_kv[tile]])

        def SCAN(h):
            def kbuf(tile):
                return B_k[0] if tile < 2 else B_k[1 + (tile - 2) // 4]

            steps = []
            groups = []
            for d_ in range(2):
                if d_ == 0:
                    glist = [(0, 2, [0, 1])] + [(2 + 4 * g, 4, [2 + 4 * g + i for i in range(4)]) for g in range(4)]
                else:
                    glist = [(0, 2, [1, 0])] + [(2 + 4 * g, 4, [2 + 4 * g + i for i in (3, 2, 1, 0)]) for g in (3, 2, 1, 0)]
                n_in_dir = 0
                for (t0, n, tiles) in glist:
                    gi = len(groups)
                    groups.append((d_, t0, n))
                    for tile in tiles:
                        steps.append([d_, tile, gi, tile - t0, n_in_dir == 0, False])
                        n_in_dir += 1
                steps[-1][5] = True
            bku_of = {}
            bka_of = {}

            gst = {}

            def G_a(gi):
                d_, t0, n = groups[gi]
                g = gi % 2
                N = n * 128
                bkx, bbx = self.bank()
                for i in range(n):
                    tsl = slice((t0 + i) * 128, (t0 + i + 1) * 128)
                    self.mm(bkx[:, i * 128:(i + 1) * 128], lrT[d_][0:17, tsl], upb[d_][0:17, h * 128:(h + 1) * 128], True, True,
                            [B_lr[d_], CONST[3]], [bbx])
                self.act(e1[:, 0:N], bkx[:, 0:N], AF.Exp, [bbx], [B_e1], scale=-1.0)
                self.act(sp[g][:, 0:N], e1[:, 0:N], AF.Ln, [B_e1], [B_sp[g]], bias=1.0)

            def G_b(gi):
                d_, t0, n = groups[gi]
                g = gi % 2
                N = n * 128
                lat = t0 >= 2
                TRI = tri_r[0] if d_ == 0 else tri_r[1]
                KLM = tri_r[2] if d_ == 0 else tri_r[3]
                lastcol = 127 if d_ == 0 else 0
                bkk, bbk = self.bank()
                for i in range(n):
                    self.mm(bkk[:, i * 128:(i + 1) * 128], KLM, sp[g][:, i * 128:(i + 1) * 128], True, True, [B_c2, B_sp[g]], [bbk])
                self.act(ekl[:, 0:N], bkk[:, 0:N], AF.Exp, [bbk], [B_ekl])
                bkb, bbb = self.bank()
                for i in range(n):
                    self.mm(bkb[:, i * 128:(i + 1) * 128], sp[g][:, i * 128:(i + 1) * 128], TRI, True, True, [B_c2, B_sp[g]], [bbb])
                self.act(ebl[g][:, 0:n], bkb[:, 0:N].rearrange("p (n l) -> p n l", l=128)[:, :, lastcol], AF.Exp, [bbb], [B_ebl[g]])
                if lat:
                    self.act(ebT[:, 0:N], bkb[:, 0:N], AF.Exp, [bbb], [B_ebT])
                    self.act(enbT[:, 0:N], bkb[:, 0:N], AF.Exp, [bbb], [B_enbT], scale=-1.0)

            def G_c(gi):
                d_, t0, n = groups[gi]
                g = gi % 2
                N = n * 128
                lat = t0 >= 2
                self.tt(kl[g][:, 0:N].rearrange("p (n d) -> p n d", d=128), kv_tm[:, t0:t0 + n, 0:128],
                        ekl[:, 0:N].rearrange("p (n d) -> p n d", d=128), ALU.mult, B_kv[t0:t0 + n] + [B_ekl], [B_kl[g]])
                if lat:
                    c0 = t0 - 2
                    self.tt(qeT[g][:, 0:N], qT[:, c0 * 128:c0 * 128 + N], ebT[:, 0:N], ALU.mult, [B_q[c0 // 4], B_ebT], [B_qeT[g]])
                    self.tt(keT[g][:, 0:N], kT[:, t0 * 128:t0 * 128 + N], enbT[:, 0:N], ALU.mult, [kbuf(t0), B_enbT], [B_keT[g]])

            def ST1(si):
                d_, tile, gi, i, first, last = steps[si]
                g = gi % 2
                j = si % 2
                MASK = mask_f if d_ == 0 else mask_b
                isl = slice(i * 128, (i + 1) * 128)
                bku, bbu = self.bank(pin=True)
                self.mm(bku[:, 0:256], kl[g][:, isl], kv_tm[:, tile, 128:384], True, True, [B_kl[g], B_kv[tile]], [bbu])
                bku_of[si] = (bku, bbu)
                if tile >= 2:
                    bka, bba = self.bank()
                    self.mm(bka[:, 0:128], keT[g][:, isl], qeT[g][:, isl], True, True, [B_keT[g], B_qeT[g]], [bba])
                    self.tt(A_sb[j], bka[:, 0:128], MASK, ALU.mult, [bba, B_c2], [B_A[j]])

            sbi = [0]
            spp = [0]

            def ST2(si):
                d_, tile, gi, i, first, last = steps[si]
                g = gi % 2
                j = si % 2
                c = tile - 2
                isl = slice(i * 128, (i + 1) * 128)
                if first:
                    self.memset(S_f[spp[0]], 0.0, [B_Sf[spp[0]]], eng="vector")
                    self.memset(S_bf[sbi[0]], 0.0, [B_Sbf[sbi[0]]], eng="vector")
                bku, bbu = bku_of.pop(si)
                if tile >= 2:
                    bko, bbo = self.bank(pin=True)
                    self.mm(bko[:, 0:256], A_sb[j], kv_tm[:, tile, 128:384], True, False, [B_A[j], B_kv[tile]], [bbo])
                    self.mm(bko[:, 0:256], qeT[g][:, isl], S_bf[sbi[0]], False, True, [B_qeT[g], B_Sbf[sbi[0]]], [bbo])
                if not last:
                    nxt = sbi[0] ^ 1
                    p0 = spp[0]
                    p1 = p0 ^ 1
                    self.stt(S_f[p1], S_f[p0], ebl[g][:, i:i + 1], bku[:, 0:256], ALU.mult, ALU.add, [B_Sf[p0], B_ebl[g], bbu], [B_Sf[p1]])
                    self.cp(S_bf[nxt], S_f[p1], [B_Sf[p1]], [B_Sbf[nxt]], eng="scalar")
                    spp[0] = p1
                    sbi[0] = nxt
                self.unpin(bku)
                if len(pend_o) > 0:
                    OEVAC()
                if tile >= 2:
                    pend_o.append((d_, c, bko, bbo))

            pend_o = []

            def OEVAC():
                d_, c, bko, bbo = pend_o.pop(0)
                if d_ == 0:
                    self.cp(o_f[:, c, :], bko[:, 0:256], [bbo], [B_of[c]], eng="vector")
                else:
                    self.tt(o_f[:, c, :], bko[:, 0:256], o_f[:, c, :], ALU.add, [bbo, B_of[c]], [B_of[c]])
                self.unpin(bko)

            G_a(0)
            G_b(0)
            G_c(0)
            ST1(0)
            pos = 0
            for si in range(len(steps)):
                gi = steps[si][2]
                n_g = groups[gi][2]
                pos = 0 if (si == 0 or steps[si - 1][2] != gi) else pos + 1
                if gi + 1 < len(groups):
                    if pos == 0:
                        G_a(gi + 1)
                    if pos == min(1, n_g - 1):
                        G_b(gi + 1)
                    if pos == min(2, n_g - 1):
                        G_c(gi + 1)
                if si + 1 < len(steps):
                    ST1(si + 1)
                ST2(si)
            while pend_o:
                OEVAC()

        on_v = [self.view(R3 + 9216 + c * 1024, BF16, (256,)) for c in range(16)]

        def F_ACT(h):
            for c in range(16):
                self.act(onj, o_f[:, c, :], AF.Square, [B_of[c]], [B_fin], accum=fin[:, c:c + 1])
            self.act(fin[:, 16:32], fin[:, 0:16], AF.Ln, [B_fin], [B_fin], bias=EPS, scale=1.0 / 256)
            self.act(fin[:, 32:48], fin[:, 16:32], AF.Exp, [B_fin], [B_fin], scale=-0.5)
            for c in range(16):
                self.act(on_v[c], o_f[:, c, :], AF.Identity, [B_of[c], B_fin], [B_of[c]], scale=fin[:, 32 + c:33 + c])

        def F_PE(h):
            for gq in range(4):
                for e_ in range(2):
                    bkt, bbt = self.bank()
                    bktb = bkt.bitcast(BF16)
                    for i in range(4):
                        c = gq * 4 + i
                        self.tr(bktb[:, i * 128:(i + 1) * 128], on_v[c][:, e_ * 128:(e_ + 1) * 128], [B_of[c], B_c2], [bbt], signal=(i == 3))
                    blk = h * 2 + e_
                    self.ts(ogT[:, blk, gq * 512:(gq + 1) * 512], bktb[:, 0:512], vcol(VC_GNG + e_), None,
                            ALU.mult, ALU.bypass, [bbt, CONST[1]], B_og[blk][gq * 4:(gq + 1) * 4])

        hw[0] = load_head(0)
        PROJ(0)
        for h in range(4):
            if h < 3:
                hw[h + 1] = load_head(h + 1)
            SCAN(h)
            F_ACT(h)
            if h < 3:
                PROJ(h + 1)
            F_PE(h)
        self.prefetch(("r", 0), [(w_in, R, 512)])
        self.prefetch(("r", 1), [(w_in, R + 512, 512)])
        C2_BUFS = B_lr + B_of + B_q + B_k + B_kv + B_kl + B_qeT + B_keT + B_A + B_ebl + B_Sf + B_Sbf + [B_e1, B_ekl, B_ebT, B_enbT, B_fin]

        if self.stop == "C2":
            return True
        sgg = [self.view(R3 + i * 2 * KB, F32, (512,)) for i in range(2)]
        tmp = [self.view(R3 + 4 * KB + i * 2 * KB, F32, (512,)) for i in range(2)]
        B_sgg, B_tmp = mk("sgg"), mk("tmp")
        srt = [self.view(R3 + 8 * KB + i * KB, BF16, (512,)) for i in range(2)]
        B_srt = mk("srt")
        self.inherit(B_sgg + B_tmp + B_srt, C2_BUFS)
        i = 0
        for half in range(2):
            wr = self.wget(("r", half), [(w_in, R + half * 512, 512)])
            for q4 in range(4):
                blk = half * 4 + q4
                for t in range(4):
                    bk, bb = self.bank()
                    proj_fm(bk, bb, wr[0], wr[1], q4 * 128, 128, 256 + t * 512, 512, LAT[t])
                    j = i % 2
                    i += 1
                    self.act(srt[j], bk, AF.Silu, [bb, CONST[1]], [B_srt[j]], bias=vcol(VC_BR + blk))
                    osl = ogT[:, blk, t * 512:(t + 1) * 512]
                    self.tt(osl, osl, srt[j], ALU.mult, B_og[blk][t * 4:(t + 1) * 4] + [B_srt[j]], B_og[blk][t * 4:(t + 1) * 4])
        i = 0
        for half in range(2):
            wgp = self.wload([(self.gla_proj, half * 512, 512)])
            wmg = self.wload([(w_in, MGG + half * 512, 512)])
            for q4 in range(4):
                nbk = half * 4 + q4
                for t in range(4):
                    bkc, bbc = self.bank()
                    for blk in range(8):
                        self.mm(bkc, wgp[0][:, blk, q4 * 128:(q4 + 1) * 128], ogT[:, blk, t * 512:(t + 1) * 512], blk == 0, blk == 7,
                                [wgp[1]] + B_og[blk][t * 4:(t + 1) * 4], [bbc])
                    bkm, bbm = self.bank()
                    proj_fm(bkm, bbm, wmg[0], wmg[1], q4 * 128, 128, 256 + t * 512, 512, LAT[t])
                    j = i % 2
                    i += 1
                    self.act(sgg[j], bkm, AF.Sigmoid, [bbm, CONST[1]], [B_sgg[j]], bias=vcol(VC_BMGG + nbk))
                    self.tt(tmp[j], bkc, sgg[j], ALU.mult, [bbc, B_sgg[j]], [B_tmp[j]])
                    msl = m1T[:, nbk, t * 512:(t + 1) * 512]
                    self.tt(msl, tmp[j], msl, ALU.add, [B_tmp[j], B_m1[nbk][t]], [B_m1[nbk][t]])
        self.prefetch(("wo", 0), [(self.w_out, 0, 512)])
        C3_BUFS = [b for row in B_og for b in row] + B_sgg + B_tmp + B_srt

        if self.stop == "C3":
            return True
        xs = [self.view(RY + i * 4 * KB, F32, (1024,)) for i in range(3)]
        hh = [self.view(RY + 12 * KB + i * 4 * KB, F32, (1024,)) for i in range(3)]
        ot = [self.view(RY + 24 * KB + i * 4 * KB, F32, (1024,)) for i in range(3)]
        junk = self.view(RY + 36 * KB, BF16, (1024,))
        B_xs = [Buf("xs%d" % i) for i in range(3)]
        B_hh = [Buf("hh%d" % i) for i in range(3)]
        B_ot = [Buf("ot%d" % i) for i in range(3)]
        B_fsL = [Buf("fs%d" % i) for i in range(16)]
        B_fng = Buf("fng")
        B_junk4 = Buf("junk4")
        self.inherit(B_xs + B_hh + B_ot + [B_fng, B_junk4], C3_BUFS + C2_BUFS)
        gate_b = C["gate_b"][bi]
        fng_b = self.view(RY + 38 * KB, F32, (1024,))
        S.dma("sync", lambda e: e.dma_start(out=fng_b, in_=self.fng_d), C["d_const"], writes=[B_fng])
        wo = [self.wget(("wo", half), [(self.w_out, half * 512, 512)]) for half in range(2)]
        d_out3 = C["d_out"] + [C["d_out2"]]

        def c4_A(c):
            x_t, bx = xs[c % 3], B_xs[c % 3]
            S.dma("sync", lambda e, x_t=x_t, c=c: e.dma_start(out=x_t, in_=self.xcat[bi, TC + c * 128:TC + (c + 1) * 128, :]),
                  C["d_x"][c % 3], writes=[bx])
            j = c % 3
            for half in range(2):
                bk, bb = self.bank()
                for cb in range(8):
                    self.mm(bk, m1T[:, cb, c * 128:(c + 1) * 128], wo[half][0][:, cb, :], cb == 0, cb == 7,
                            [wo[half][1], B_m1[cb][c // 4]], [bb])
                hs = hh[j][:, half * 512:(half + 1) * 512]
                self.tt(hs, bk, gate_b[:, half * 512:(half + 1) * 512], ALU.mult, [bb, B_c2], [B_hh[j]])
            self.tt(hh[j], hh[j], x_t, ALU.add, [B_hh[j], bx], [B_hh[j]], eng="gpsimd")

        def c4_B(c):
            j = c % 3
            B_fs = B_fsL[c]
            fs = small[:, 96 + 4 * (c % 8):100 + 4 * (c % 8)]
            self.act(junk, hh[j], AF.Square, [B_hh[j]], [B_fs, B_junk4], accum=fs[:, 0:1])
            self.act(fs[:, 1:2], fs[:, 0:1], AF.Ln, [B_fs], [B_fs], bias=EPS, scale=1.0 / D)
            self.act(fs[:, 2:3], fs[:, 1:2], AF.Exp, [B_fs], [B_fs], scale=-0.5)

        def c4_C(c):
            j = c % 3
            B_fs = B_fsL[c]
            fs = small[:, 96 + 4 * (c % 8):100 + 4 * (c % 8)]
            self.stt(ot[j], hh[j], fs[:, 2:3], fng_b, ALU.mult, ALU.mult, [B_hh[j], B_fs, B_fng], [B_ot[j]])
            o_t = ot[j]
            S.dma("sync", lambda e, o_t=o_t, c=c: e.dma_start(out=self.out_d[bi, c * 128:(c + 1) * 128, :], in_=o_t),
                  d_out3[j], reads=[B_ot[j]])

        for i in range(16 + 2):
            if i < 16:
                c4_A(i)
            if 0 <= i - 1 < 16:
                c4_B(i - 1)
            if 0 <= i - 2 < 16:
                c4_C(i - 2)
        S.barrier()
        self.prev = None
        return self.stop == "C4"


def _prep_shared(inp):
    f = lambda a: np.ascontiguousarray(np.asarray(a, dtype=np.float32))
    b_in = f(inp["b_in"])[0]
    vec = np.zeros((128, NV), np.float32)

    def fm(v, nblk):
        return v.reshape(nblk, 128).T

    ada_b = f(inp["ada_b"])[0]
    vec[:, VC_ADAB:VC_ADAB + 16] = fm(ada_b[0:2048], 16)
    vec[:, VC_NG:VC_NG + 8] = fm(f(inp["norm_g"])[0], 8)
    vec[:, VC_BGV:VC_BGV + 8] = fm(b_in[GV:GV + 1024], 8)
    vec[:, VC_BGG:VC_BGG + 8] = fm(b_in[GG:GG + 1024], 8)
    vec[:, VC_BZ:VC_BZ + 8] = fm(b_in[Z:Z + 1024], 8)
    vec[:, VC_BQ:VC_BQ + 4] = fm(b_in[Q:Q + 512], 4)
    vec[:, VC_BK:VC_BK + 4] = fm(b_in[K:K + 512], 4)
    vec[:, VC_BR:VC_BR + 8] = fm(b_in[R:R + 1024], 8)
    vec[:, VC_BMGC:VC_BMGC + 8] = fm(b_in[MGC:MGC + 1024], 8)
    vec[:, VC_BMGG:VC_BMGG + 8] = fm(b_in[MGG:MGG + 1024], 8)
    vec[:, VC_CB:VC_CB + 8] = fm(f(inp["conv_b"])[0], 8)
    vec[:, VC_LNG:VC_LNG + 8] = fm(f(inp["conv_ln_g"])[0], 8)
    vec[:, VC_LNB:VC_LNB + 8] = fm(f(inp["conv_ln_b"])[0], 8)
    vec[:, VC_GNG:VC_GNG + 2] = fm(f(inp["gla_norm_g"])[0], 2)
    vec[0:16, VC_BAF] = b_in[AFO:AFO + 16]
    vec[0:16, VC_BAB] = b_in[ABO:ABO + 16]
    cw = f(inp["conv_w"])[0]
    cwp = np.concatenate([cw, np.zeros((1, 1024), np.float32)], axis=0)
    cw5 = cwp.reshape(16, 2, 8, 2, 64)
    vec[:, VC_CW:VC_CW + 256] = cw5.transpose(1, 4, 2, 3, 0).reshape(128, 256)
    kvrow = np.concatenate([np.concatenate([b_in[K + h * 128:K + (h + 1) * 128], b_in[V + h * 256:V + (h + 1) * 256]]) for h in range(4)])
    kvb = np.ascontiguousarray(np.broadcast_to(kvrow[None, :], (128, 1536)))
    adag = np.ascontiguousarray(np.broadcast_to(ada_b[None, 2048:3072], (128, 1024)))
    fng = np.ascontiguousarray(np.broadcast_to(f(inp["final_norm_g"])[None, :], (128, 1024)))
    idx = np.arange(128)
    le = (idx[:, None] <= idx[None, :]).astype(np.float32)
    ge = (idx[:, None] >= idx[None, :]).astype(np.float32)
    gt = (idx[:, None] > idx[None, :]).astype(np.float32)
    lt = (idx[:, None] < idx[None, :]).astype(np.float32)
    s = np.float32(-1.0 / 16.0)
    si = np.concatenate([np.eye(64, dtype=np.float32), np.eye(64, dtype=np.float32)], axis=0)
    consts = np.concatenate([np.eye(128, dtype=np.float32), le, ge, s * le, s * ge, s * gt, s * lt, si], axis=1)
    upb = np.stack([
        np.concatenate([f(inp["decay_up_fwd"])[0], f(inp["decay_bias_fwd"])[0][None, :]], axis=0),
        np.concatenate([f(inp["decay_up_bwd"])[0], f(inp["decay_bias_bwd"])[0][None, :]], axis=0)], axis=0)
    shared = dict(kvb=kvb, adag=adag, fng=fng, consts=np.ascontiguousarray(consts), upb=np.ascontiguousarray(upb),
                  ada_w=f(inp["ada_w"])[0], w_in=f(inp["w_in"])[0], conv_proj=f(inp["conv_proj"])[0],
                  gla_proj=f(inp["gla_proj"])[0], w_out=f(inp["w_out"])[0])
    return vec, shared


_NC_CACHE = {}


def _get_nc(nb):
    if nb not in _NC_CACHE:
        p = Prog(nb=nb)
        _NC_CACHE[nb] = p.build()
    return _NC_CACHE[nb]


def kernel(**inp):
    x = np.asarray(inp["x"], dtype=np.float32)
    ctx = np.asarray(inp["ctx"], dtype=np.float32)
    c = np.asarray(inp["c"], dtype=np.float32)
    c_ctx = np.asarray(inp["c_ctx"], dtype=np.float32)
    Bn = x.shape[0]
    ncores = 8
    nb = Bn // ncores
    vec0, shared = _prep_shared(inp)
    in_maps = []
    for core in range(ncores):
        b0 = core * nb
        xcat = np.ascontiguousarray(np.concatenate([ctx[b0:b0 + nb], x[b0:b0 + nb]], axis=1))
        vec = vec0.copy()
        cols = np.stack([c[b0 + i] if i < nb else c_ctx for i in range(3)] if nb == 2 else
                        [c[b0], c[b0], c_ctx], axis=1)
        vec[:, VC_C:VC_C + 24] = cols.reshape(8, 128, 3).transpose(1, 0, 2).reshape(128, 24)
        m = dict(shared)
        m["xcat"] = xcat
        m["vecs"] = vec
        in_maps.append(m)
    nc = _get_nc(nb)
    res = run_bass_kernel_spmd(nc, in_maps, core_ids=list(range(ncores)))
    out = np.concatenate([np.asarray(r["out"]) for r in res.results], axis=0)
    return out.astype(np.float32)
```

```python
import numpy as np
import ml_dtypes
from contextlib import ExitStack
import concourse.bass as bass
import concourse.mybir as mybir
from concourse.bass_utils import run_bass_kernel_spmd

F32 = mybir.dt.float32
F32R = mybir.dt.float32r
BF16 = mybir.dt.bfloat16
AF = mybir.ActivationFunctionType
ALU = mybir.AluOpType

D = 1024
T = 2048
TC = 256
TT = T + TC
NT = TT // 128
EPS = 1e-6
GV, GG, Z, Q, K, V, AFO, ABO, R, MGC, MGG = 0, 1024, 2048, 3072, 3584, 4096, 5120, 5136, 5152, 6176, 7200
NIN = 8224
QSCALE = 128 ** -0.5

VC_C = 0
VC_ADAB = 24
VC_NG = 40
VC_BGV, VC_BGG, VC_BZ = 48, 56, 64
VC_BQ, VC_BK = 72, 76
VC_BR, VC_BMGC, VC_BMGG = 80, 88, 96
VC_CB, VC_LNG, VC_LNB = 104, 112, 120
VC_GNG = 128
VC_BAF, VC_BAB = 130, 131
VC_CW = 132
NV = 388


class DSem:
    def __init__(self, sem):
        self.sem = sem
        self.count = 0


class Buf:
    __slots__ = ("name", "last_w", "readers")

    def __init__(self, name=""):
        self.name = name
        self.last_w = None
        self.readers = []


class EngState:
    def __init__(self, name, sem):
        self.name = name
        self.sem = sem
        self.count = 0
        self.seen = {}
        self.prog = []


class Sched:
    def __init__(self, nc, stack):
        self.nc = nc
        self.stack = stack
        self.engs = {}
        for n in ["tensor", "vector", "scalar", "gpsimd", "sync"]:
            sem = stack.enter_context(nc.semaphore("s_" + n))
            self.engs[n] = EngState(n, sem)
        self.dsems = []
        self.ninst = 0

    def new_dsem(self):
        d = DSem(self.stack.enter_context(self.nc.semaphore("d%d" % len(self.dsems))))
        self.dsems.append(d)
        return d

    def _waits(self, eng, reads, writes, dsem=None):
        deps = []
        for b in reads:
            if b.last_w is not None:
                deps.append(b.last_w)
        for b in writes:
            if b.last_w is not None:
                if not (dsem is not None and b.last_w[0] is dsem.sem):
                    deps.append(b.last_w)
            deps.extend(b.readers)
        best = {}
        for (sem, val) in deps:
            if sem is eng.sem and eng.name == "tensor":
                continue
            k = id(sem)
            if eng.seen.get(k, 0) >= val:
                continue
            if k not in best or best[k][1] < val:
                best[k] = (sem, val)
        for k, (sem, val) in best.items():
            eng.seen[k] = val
        return list(best.values())

    def emit(self, engname, fn, reads=(), writes=(), signal=True):
        eng = self.engs[engname]
        waits = self._waits(eng, reads, writes)
        if signal:
            eng.count += 1
            tok = (eng.sem, eng.count)
        else:
            assert engname == "tensor"
            tok = (eng.sem, eng.count + 1)
        sem = eng.sem

        def run(e, waits=waits, fn=fn, signal=signal, sem=sem):
            for (s, v) in waits:
                e.wait_ge(s, v)
            inst = fn(e)
            if signal:
                inst.then_inc(sem, 1)

        eng.prog.append(run)
        self.ninst += 1
        for b in writes:
            b.last_w = tok
            b.readers = []
        for b in reads:
            if b not in writes:
                b.readers.append(tok)
        return tok

    def dma(self, qname, fn, dsem, reads=(), writes=()):
        eng = self.engs[qname]
        waits = self._waits(eng, reads, writes, dsem=dsem)
        dsem.count += 16
        tok = (dsem.sem, dsem.count)

        def run(e, waits=waits, fn=fn, s=dsem.sem):
            for (ws, v) in waits:
                e.wait_ge(ws, v)
            fn(e).then_inc(s, 16)

        eng.prog.append(run)
        self.ninst += 1
        for b in writes:
            b.last_w = tok
            b.readers = []
        for b in reads:
            if b not in writes:
                b.readers.append(tok)
        return tok

    def barrier(self):
        toks = [(e.sem, e.count) for e in self.engs.values() if e.count > 0]
        toks += [(d.sem, d.count) for d in self.dsems if d.count > 0]
        for eng in self.engs.values():
            w = []
            for (s, v) in toks:
                if s is eng.sem and eng.name == "tensor":
                    continue
                if eng.seen.get(id(s), 0) < v:
                    eng.seen[id(s)] = v
                    w.append((s, v))
            if w:
                def run(e, w=w):
                    for (s, v) in w:
                        e.wait_ge(s, v)
                eng.prog.append(run)

    def wait_all(self, engname, toks):
        eng = self.engs[engname]

        def run(e, toks=list(toks)):
            for (s, v) in toks:
                e.wait_ge(s, v)

        eng.prog.append(run)

    def build(self):
        nc = self.nc
        with nc.Block() as block:
            for n, dec in [("tensor", block.tensor), ("vector", block.vector), ("scalar", block.scalar),
                           ("gpsimd", block.gpsimd), ("sync", block.sync)]:
                prog = self.engs[n].prog

                def body(e, prog=prog):
                    for r in prog:
                        r(e)

                dec(body)


class Prog:
    def __init__(self, nb=2, dbg=None, stop=None, nodump=False):
        self.nodump = nodump
        self.nb = nb
        self.dbg = dbg
        self.stop = stop
        self.layout = {}
        self.nc = bass.Bass("TRN2", target_bir_lowering=False, dynamic_dma_scratch_size=4096)

    def mm(self, out, lhsT, rhs, start, stop, reads, writes, sig=False):
        self.S.emit("tensor", lambda e: e.matmul(out, lhsT=lhsT, rhs=rhs, start=start, stop=stop),
                    reads=reads, writes=writes, signal=(stop or sig))

    def tr(self, out, in_, reads, writes, signal=True):
        ident = self.ident_bf
        self.S.emit("tensor", lambda e: e.transpose(out, in_, ident), reads=reads, writes=writes, signal=signal)

    def act(self, out, in_, func, reads, writes, bias=None, scale=None, accum=None):
        kw = {}
        if bias is not None:
            kw["bias"] = bias
        if scale is not None:
            kw["scale"] = scale
        if accum is not None:
            kw["accum_out"] = accum
        self.S.emit("scalar", lambda e: e.activation(out=out, in_=in_, func=func, **kw), reads=reads, writes=writes)

    def tt(self, out, in0, in1, op, reads, writes, eng="vector"):
        self.S.emit(eng, lambda e: e.tensor_tensor(out=out, in0=in0, in1=in1, op=op), reads=reads, writes=writes)

    def stt(self, out, in0, scalar, in1, op0, op1, reads, writes):
        self.S.emit("vector", lambda e: e.scalar_tensor_tensor(out=out, in0=in0, scalar=scalar, in1=in1, op0=op0, op1=op1),
                    reads=reads, writes=writes)

    def ts(self, out, in0, s1, s2, op0, op1, reads, writes, eng="vector"):
        self.S.emit(eng, lambda e: e.tensor_scalar(out=out, in0=in0, scalar1=s1, scalar2=s2, op0=op0, op1=op1),
                    reads=reads, writes=writes)

    def cp(self, out, in_, reads, writes, eng="vector"):
        if eng == "scalar":
            self.S.emit(eng, lambda e: e.copy(out=out, in_=in_), reads=reads, writes=writes)
        else:
            self.S.emit(eng, lambda e: e.tensor_copy(out=out, in_=in_), reads=reads, writes=writes)

    def memset(self, ap, val, writes, eng="gpsimd"):
        self.S.emit(eng, lambda e: e.memset(ap, val), reads=(), writes=writes)

    def recip(self, out, in_, reads, writes):
        self.S.emit("vector", lambda e: e.reciprocal(out=out, in_=in_), reads=reads, writes=writes)

    def view(self, off, dtype, shape, parts=128, name=None):
        if name is not None:
            self.layout[name] = (off, "f32" if dtype in (F32, F32R) else "bf16", tuple(shape), parts)
        esz = 4 if dtype in (F32, F32R) else 2
        n = int(np.prod(shape))
        nbytes = n * esz
        assert off % 4 == 0 and nbytes % 4 == 0
        assert off + nbytes <= self.arena_bytes, (off, nbytes, self.arena_bytes)
        ap = self.arena[0:parts, off // 4:(off + nbytes) // 4]
        if dtype != F32:
            ap = ap.bitcast(dtype)
        if len(shape) == 2:
            ap = ap.rearrange("p (a b) -> p a b", a=shape[0])
        elif len(shape) == 3:
            ap = ap.rearrange("p (a b c) -> p a b c", a=shape[0], b=shape[1])
        return ap

    def wload(self, pieces):
        i = self.wslot_i
        self.wslot_i = (i + 1) % len(self.wslots)
        view, buf, dsem = self.wslots[i]
        c = 0
        for (src, col0, ncols) in pieces:
            srcv = src[:, col0:col0 + ncols].rearrange("(kb p) n -> p kb n", p=128)
            dst = view[:, :, c:c + ncols]
            self.S.dma("gpsimd", lambda e, dst=dst, srcv=srcv: e.dma_start(out=dst, in_=srcv), dsem, writes=[buf])
            c += ncols
        assert c <= 512
        return view, buf

    @staticmethod
    def inherit(new_bufs, old_bufs):
        for n in new_bufs:
            for o in old_bufs:
                n.readers.extend(o.readers)
                if o.last_w is not None:
                    n.readers.append(o.last_w)

    def prefetch(self, key, pieces):
        self.pref[key] = self.wload(pieces)

    def wget(self, key, pieces):
        if key in self.pref:
            return self.pref.pop(key)
        return self.wload(pieces)

    def bank(self, pin=False):
        i = self.bank_i
        while i in self.pinned:
            i = (i + 1) % 8
        self.bank_i = (i + 1) % 8
        if pin:
            self.pinned.add(i)
        return self.banks[i], self.bank_bufs[i]

    def unpin(self, bk):
        for i in range(8):
            if self.banks[i] is bk:
                self.pinned.discard(i)

    def build(self):
        nc = self.nc
        nb = self.nb
        dt = nc.dram_tensor
        self.xcat = dt("xcat", [nb, TT, D], F32, kind="ExternalInput").ap()
        self.vecs_d = dt("vecs", [128, NV], F32, kind="ExternalInput").ap()
        self.kvb_d = dt("kvb", [128, 1536], F32, kind="ExternalInput").ap()
        self.adag_d = dt("adag", [128, 1024], F32, kind="ExternalInput").ap()
        self.fng_d = dt("fng", [128, 1024], F32, kind="ExternalInput").ap()
        self.consts_d = dt("consts", [128, 7 * 128 + 64], F32, kind="ExternalInput").ap()
        self.upb_d = dt("upb", [2, 17, 512], F32, kind="ExternalInput").ap()
        self.ada_w = dt("ada_w", [D, 3 * D], F32, kind="ExternalInput").ap()
        self.w_in = dt("w_in", [D, NIN], F32, kind="ExternalInput").ap()
        self.conv_proj = dt("conv_proj", [D, D], F32, kind="ExternalInput").ap()
        self.gla_proj = dt("gla_proj", [D, D], F32, kind="ExternalInput").ap()
        self.w_out = dt("w_out", [D, D], F32, kind="ExternalInput").ap()
        self.out_d = dt("out", [nb, T, D], F32, kind="ExternalOutput").ap()
        if self.stop:
            self.dbg_d = dt("dbg", [128, (219 * 1024 - 6656) // 4], F32, kind="ExternalOutput").ap()

        with ExitStack() as st:
            self.S = S = Sched(nc, st)
            KB = 1024
            self.arena_bytes = 219 * KB - 6656
            self.arena = st.enter_context(nc.sbuf_tensor("arena", [128, self.arena_bytes // 4], F32))[:, :]
            self.banks = [st.enter_context(nc.psum_tensor("bank%d" % i, [128, 512], F32))[:, :] for i in range(8)]
            self.bank_bufs = [Buf("bank%d" % i) for i in range(8)]
            self.bank_i = 0
            self.pinned = set()

            o = 0

            def alloc(nbytes):
                nonlocal o
                r = o
                o += (nbytes + 31) // 32 * 32
                return r

            self.ident_bf = self.view(alloc(256), BF16, (128,))
            ones_bf = self.view(alloc(256), BF16, (128,))
            self.si_bf = self.view(alloc(128), BF16, (64,))
            ident_f = self.view(alloc(512), F32, (128,))
            mask_f = self.view(alloc(512), F32, (128,))
            mask_b = self.view(alloc(512), F32, (128,))
            rt = lambda name, n: st.enter_context(nc.sbuf_tensor(name, [128, n], F32R))[:, :]
            tri_r = [rt("tri_r%d" % i, 128) for i in range(4)]
            ones_r = rt("ones_r", 128)
            self.ysq_r = [rt("ysq_r%d" % i, 512) for i in range(2)]
            ones_f = self.view(alloc(512), F32, (128,))
            vecs = self.view(alloc(NV * 4), F32, (NV,))
            self.vecs = vecs
            kvb_b = self.view(alloc(1536 * 4), F32, (1536,))
            upb = [self.view(alloc(1024), BF16, (512,), parts=17) for _ in range(2)]
            gate_b = [self.view(alloc(4096), F32, (1024,), name='gate_b%d' % i) for i in range(2)]
            gs = self.view(alloc(96), F32, (3, 8), name='gs')
            sh = self.view(alloc(96), F32, (3, 8), name='sh')
            sc_f = self.view(alloc(96), F32, (24,))
            sc_bf = self.view(alloc(64), BF16, (24,))
            small = self.view(alloc(1024), F32, (256,))
            uT = self.view(alloc(8 * TT * 2), BF16, (8, TT), name='uT')
            self.uT = uT
            self.wslots = []
            for i in range(3):
                v = self.view(alloc(8 * 512 * 2), BF16, (8, 512))
                self.wslots.append((v, Buf("w%d" % i), S.new_dsem()))
            self.wslot_i = 0
            self.pref = {}
            DYN = o
            RY = DYN
            R2 = DYN + 64 * KB
            R3 = DYN + 96 * KB
            R3SZ = self.arena_bytes - R3
            assert R3SZ >= 34 * KB, R3SZ

            B_const = Buf("const")
            B_vecs = Buf("vecs")
            d_const = S.new_dsem()
            d_x = [S.new_dsem() for _ in range(3)]
            d_x3 = S.new_dsem()
            d_out = [S.new_dsem() for _ in range(2)]
            d_out2 = S.new_dsem()

            cst_all = self.view(RY, F32, (7 * 128 + 64,))
            cst = cst_all[:, 0:7 * 128].rearrange("p (a b) -> p a b", a=7)
            B_cst = Buf("cst")
            S.dma("sync", lambda e: e.dma_start(out=cst_all, in_=self.consts_d), d_const, writes=[B_cst])
            S.dma("sync", lambda e: e.dma_start(out=vecs, in_=self.vecs_d), d_const, writes=[B_vecs])
            S.dma("sync", lambda e: e.dma_start(out=kvb_b, in_=self.kvb_d), d_const, writes=[B_const])
            adag_b = self.view(RY + 4 * KB, F32, (1024,))
            S.dma("sync", lambda e: e.dma_start(out=adag_b, in_=self.adag_d), d_const, writes=[B_const])
            d_upb = S.new_dsem()
            B_upb = Buf("upb")
            for i in range(2):
                S.dma("gpsimd", lambda e, i=i: e.dma_start(out=upb[i], in_=self.upb_d[i]), d_upb, writes=[B_upb])
            tot = (d_const.sem, d_const.count)
            B_cst.last_w = tot
            B_vecs.last_w = tot
            B_const.last_w = tot
            B_upb.last_w = (d_upb.sem, d_upb.count)
            B_c2 = Buf("c2")
            self.cp(ident_f, cst[:, 0, :], [B_cst], [B_c2])
            self.cp(self.ident_bf, cst[:, 0, :], [B_cst], [B_c2])
            self.cp(self.si_bf, cst_all[:, 7 * 128:7 * 128 + 64], [B_cst], [B_c2])
            self.cp(mask_f, cst[:, 1, :], [B_cst], [B_c2])
            self.cp(mask_b, cst[:, 2, :], [B_cst], [B_c2])
            for i in range(4):
                self.cp(tri_r[i], cst[:, 3 + i, :], [B_cst], [B_c2])
            self.memset(ones_f, 1.0, [B_c2], eng="vector")
            self.memset(ones_bf, 1.0, [B_c2], eng="vector")
            self.cp(ones_r, ones_f, [B_c2], [B_c2])
            CONST = [B_c2, B_vecs, B_const, B_upb]

            self.act(sc_f, vecs[:, VC_C:VC_C + 24], AF.Silu, [B_vecs], [B_c2])
            self.cp(sc_bf, sc_f, [B_c2], [B_c2])
            aw1 = self.view(RY + 8 * KB, BF16, (8, 2048))
            aw2 = self.view(RY + 8 * KB + 32 * KB, BF16, (8, 1024))
            scb = [self.view(RY + 56 * KB + i * 2 * KB, BF16, (8, 128)) for i in range(2)]
            B_awc = [Buf("aw%d" % c) for c in range(6)]
            d_awc = [S.new_dsem() for _ in range(6)]
            awv = self.ada_w.rearrange("(kb p) n -> p kb n", p=128)
            for c in range(4):
                S.dma("gpsimd", lambda e, c=c: e.dma_start(out=aw1[:, :, c * 512:(c + 1) * 512], in_=awv[:, :, c * 512:(c + 1) * 512]), d_awc[c], writes=[B_awc[c]])
            for c in range(2):
                S.dma("gpsimd", lambda e, c=c: e.dma_start(out=aw2[:, :, c * 512:(c + 1) * 512], in_=awv[:, :, 2048 + c * 512:2048 + (c + 1) * 512]), d_awc[4 + c], writes=[B_awc[4 + c]])
            bk, bb = self.bank()
            bk4 = bk[:, 0:64].rearrange("p (n f) -> p n f", f=4)
            for nbk in range(16):
                for kb in range(8):
                    self.mm(bk4[:, nbk, 0:3], aw1[:, kb, nbk * 128:(nbk + 1) * 128], sc_bf[:, kb * 3:(kb + 1) * 3],
                            kb == 0, kb == 7, [B_awc[nbk // 4], B_c2], [bb])
            modv = self.view(R3, F32, (3, 16))
            B_mod = Buf("mod")
            for v in range(3):
                self.tt(modv[:, v, :], bk4[:, :, v], vecs[:, VC_ADAB:VC_ADAB + 16], ALU.add, [bb, B_vecs], [B_mod])
                self.cp(sh[:, v, :], modv[:, v, 0:8], [B_mod], [B_c2])
                self.stt(gs[:, v, :], modv[:, v, 8:16], 1.0, vecs[:, VC_NG:VC_NG + 8], ALU.add, ALU.mult, [B_mod, B_vecs], [B_c2])
            for v in range(nb):
                for kb in range(8):
                    self.ts(scb[v][:, kb, :], ones_bf, sc_f[:, kb * 3 + v:kb * 3 + v + 1], None, ALU.mult, ALU.bypass, [B_c2], [B_c2])
                for half in range(2):
                    bk, bb = self.bank()
                    for kb in range(8):
                        self.mm(bk, scb[v][:, kb, :], aw2[:, kb, half * 512:(half + 1) * 512], kb == 0, kb == 7, [B_awc[4 + half], B_c2], [bb])
                    self.tt(gate_b[v][:, half * 512:(half + 1) * 512], bk, adag_b[:, half * 512:(half + 1) * 512], ALU.add,
                            [bb, B_const], [B_c2])
            S.barrier()

            for bi in range(nb):
                if self.stop == "A":
                    break
                done = self.batch(bi, RY, R2, R3, KB, CONST, dict(
                    ident_f=ident_f, ones_bf=ones_bf, mask_f=mask_f, mask_b=mask_b, tri_r=tri_r, ones_r=ones_r,
                    ones_f=ones_f, kvb_b=kvb_b, upb=upb, d_const=d_const, gate_b=gate_b, gs=gs, sh=sh, small=small,
                    d_x=d_x, d_x3=d_x3, d_out=d_out, d_out2=d_out2))
                if done:
                    break

            if self.stop and not self.nodump:
                S.barrier()
                d_dbg = S.new_dsem()
                nel = self.arena_bytes // 4
                step = nel // 4
                for i in range(4):
                    S.dma("sync", lambda e, i=i: e.dma_start(out=self.dbg_d[:, i * step:(i + 1) * step], in_=self.arena[:, i * step:(i + 1) * step]), d_dbg)
                S.wait_all("sync", [(d_dbg.sem, d_dbg.count)])
            S.wait_all("sync", [(d.sem, d.count) for d in d_out + [d_out2] if d.count > 0])
            S.build()
        return nc

    def batch(self, bi, RY, R2, R3, KB, CONST, C):
        S = self.S
        vecs = self.vecs
        uT = self.uT
        gs, sh, small = C["gs"], C["sh"], C["small"]
        B_c2 = CONST[0]
        w_in = self.w_in

        def vcol(c):
            return vecs[:, c:c + 1]

        self.prefetch(("pair", 0), [(w_in, GV, 256), (w_in, GG, 256)])
        xs = [self.view(RY + i * 4 * KB, F32, (1024,)) for i in range(3)]
        xn = [self.view(RY + 12 * KB + i * 2 * KB, BF16, (1024,)) for i in range(2)]
        junk = self.view(RY + 16 * KB, BF16, (1024,))
        B_xs = [Buf("xs%d" % i) for i in range(3)]
        B_xn = [Buf("xn%d" % i) for i in range(2)]
        B_uT = [Buf("uT%d" % i) for i in range(NT)]
        B_uTo = [Buf("uTo%d" % i) for i in range(NT)]
        B_junk = Buf("junk")
        B_smL = [Buf("small%d" % i) for i in range(NT)]
        xs.append(self.view(RY + 18 * KB, F32, (1024,)))
        B_xs.append(Buf("xs3"))
        if getattr(self, "prev", None):
            self.inherit(B_xs + B_xn + [B_junk], self.prev["c4"])
            self.inherit(B_uT + B_uTo, self.prev["uT"])
        d_x4 = C["d_x"] + [C["d_x3"]]
        pend = {}

        def stage1(tt):
            B_sm = B_smL[tt]
            x_t, bx = xs[tt % 4], B_xs[tt % 4]
            S.dma("sync", lambda e, x_t=x_t, tt=tt: e.dma_start(out=x_t, in_=self.xcat[bi, tt * 128:(tt + 1) * 128, :]),
                  d_x4[tt % 4], writes=[bx])
            ss = small[:, tt:tt + 1]
            ln_ = small[:, 32 + tt:33 + tt]
            rstd = small[:, 64 + tt:65 + tt]
            self.act(junk, x_t, AF.Square, [bx], [B_sm, B_junk], accum=ss)
            self.act(ln_, ss, AF.Ln, [B_sm], [B_sm], bias=EPS, scale=1.0 / D)
            self.act(rstd, ln_, AF.Exp, [B_sm], [B_sm], scale=-0.5)

        def stage2(tt):
            B_sm = B_smL[tt]
            x_t, bx = xs[tt % 4], B_xs[tt % 4]
            rstd = small[:, 64 + tt:65 + tt]
            xn_t, bxn = xn[tt % 2], B_xn[tt % 2]
            self.ts(xn_t, x_t, rstd, None, ALU.mult, ALU.bypass, [bx, B_sm], [bxn])
            bk, bb = self.bank(pin=True)
            bkb = bk.bitcast(BF16)
            bk2, bb2 = self.bank(pin=True)
            bkb2 = bk2.bitcast(BF16)
            for kb in range(6):
                self.tr(bkb[:, kb * 128:(kb + 1) * 128], xn_t[:, kb * 128:(kb + 1) * 128], [bxn, B_c2], [bb], signal=(kb == 5))
            for kb in range(6, 8):
                self.tr(bkb2[:, (kb - 6) * 128:(kb - 5) * 128], xn_t[:, kb * 128:(kb + 1) * 128], [bxn, B_c2], [bb2], signal=(kb == 7))
            pend[tt] = (bk, bkb, bb, bk2, bkb2, bb2)

        def stage3(tt):
            v = 2 if tt < 2 else bi
            bk, bkb, bb, bk2, bkb2, bb2 = pend.pop(tt)
            for kb in range(8):
                dst = uT[:, kb, tt * 128:(tt + 1) * 128]
                if kb < 6:
                    src = bkb[:, kb * 128:(kb + 1) * 128]
                    self.ts(dst, src, gs[:, v, kb:kb + 1], sh[:, v, kb:kb + 1], ALU.mult, ALU.add, [bb, B_c2],
                            [B_uT[tt]] if kb == 5 else [Buf("uTtmp")])
                else:
                    src = bkb2[:, (kb - 6) * 128:(kb - 5) * 128]
                    self.act(dst, src, AF.Identity, [bb2, B_c2], [B_uTo[tt]] if kb == 7 else [Buf("uTtmp")],
                             bias=sh[:, v, kb:kb + 1], scale=gs[:, v, kb:kb + 1])
            self.unpin(bk)
            self.unpin(bk2)

        for tt in range(NT + 2):
            if tt < NT:
                stage1(tt)
            if 0 <= tt - 1 < NT:
                stage2(tt - 1)
            if 0 <= tt - 2 < NT:
                stage3(tt - 2)
        PHB_BUFS = B_xs + B_xn + [B_junk]

        LAT = [B_uT[2 + 4 * i:6 + 4 * i] + B_uTo[2 + 4 * i:6 + 4 * i] for i in range(4)]

        def proj_fm(bk, bb, wv, wb, c0, ncols, tok0, ntok, rbufs):
            for kb in range(8):
                self.mm(bk[0:ncols, 0:ntok], wv[:, kb, c0:c0 + ncols], uT[:, kb, tok0:tok0 + ntok], kb == 0, kb == 7,
                        [wb] + rbufs, [bb])

        if self.stop == "B":
            return True
        y = self.view(RY, F32, (8, T), name='y')
        B_y = [[Buf("y%d_%d" % (cb, t)) for t in range(4)] for cb in range(8)]
        self.inherit([b for row in B_y for b in row], PHB_BUFS)
        LA = 3968
        a_pad = [self.view(R2 + i * 7936, BF16, (LA,), name='a_pad%d' % i) for i in range(2)]
        AS = [[self.view(R2 + 15872 + (i * 2 + q) * 7936, BF16, (LA,)) for q in range(2)] for i in range(2)]
        dg2 = [self.view(R2 + 47616 + i * 4096, BF16, (2, 16, 64)) for i in range(2)]
        sg = [self.view(R2 + 55808 + i * 2 * KB, F32, (512,)) for i in range(2)]
        B_ap = [Buf("apad%d" % i) for i in range(2)]
        B_AS = [Buf("AS%d" % i) for i in range(2)]
        B_dg = [Buf("dg%d" % i) for i in range(2)]
        B_sg = [Buf("sg%d" % i) for i in range(2)]
        if getattr(self, "prev", None):
            self.inherit(B_ap + B_AS + B_dg + B_sg, self.prev["m1"] + self.prev["c3"])
        d_as = [S.new_dsem() for _ in range(2)] if not hasattr(self, "d_as") else self.d_as
        self.d_as = d_as
        wcur = {}
        if bi == 0:
            for i in range(2):
                for q in range(2):
                    self.memset(AS[i][q], 0.0, [B_AS[i]])

        def load_pair(p):
            wcur[p] = self.wget(("pair", p), [(w_in, GV + p * 256, 256), (w_in, GG + p * 256, 256)])

        def conv_proj_stage(cb):
            wv, wb = wcur[cb // 2]
            b_ = cb % 2
            ap_t, bap = a_pad[b_], B_ap[b_]
            if cb in (0, 1, 4, 5):
                self.memset(ap_t, 0.0, [bap])
            for q in range(2):
                for m in range(16):
                    self.ts(dg2[b_][:, q, m, :], self.si_bf, vcol(VC_CW + (cb * 2 + q) * 16 + m), 0.0, ALU.mult, ALU.add,
                            [B_c2, CONST[1]], [B_dg[b_]], eng="gpsimd")
            for t in range(4):
                bkv, bbv = self.bank()
                proj_fm(bkv, bbv, wv, wb, (cb % 2) * 128, 128, 256 + t * 512, 512, LAT[t])
                bkg, bbg = self.bank()
                proj_fm(bkg, bbg, wv, wb, 256 + (cb % 2) * 128, 128, 256 + t * 512, 512, LAT[t])
                sgt, bsg = sg[t % 2], B_sg[t % 2]
                self.act(sgt, bkg, AF.Sigmoid, [bbg, CONST[1]], [bsg], bias=vcol(VC_BGG + cb))
                if cb < 4:
                    dst = ap_t[:, 0:32 * 94].rearrange("p (r w) -> p r w", w=94)[:, t * 8:(t + 1) * 8, 15:79]
                    i0 = bkv.rearrange("p (r w) -> p r w", w=64)
                    i1 = sgt.rearrange("p (r w) -> p r w", w=64)
                else:
                    dst = ap_t[:, (15 + t * 8) * 64:(15 + t * 8 + 8) * 64]
                    i0 = bkv
                    i1 = sgt
                self.stt(dst, i0, vcol(VC_BGV + cb), i1, ALU.add, ALU.mult, [bbv, bsg, CONST[1]], [bap])
            delta = 1 if cb < 4 else 64
            for q in range(2):
                for j in range(2):
                    dst = AS[b_][q][j * 64:(j + 1) * 64, 0:LA - j * delta]
                    src = ap_t[q * 64:(q + 1) * 64, j * delta:LA]
                    S.dma("sync", lambda e, dst=dst, src=src: e.dma_start(out=dst, in_=src), d_as[b_], reads=[bap], writes=[B_AS[b_]])

        def conv_stage(cb):
            b_ = cb % 2
            bas, bdg = B_AS[b_], B_dg[b_]
            for t in range(4):
                bk, bb = self.bank()
                groups_ = []
                for m in range(16):
                    s0 = 2 * m - 15
                    if cb < 4:
                        rhs = [AS[b_][q][:, 0:32 * 94].rearrange("p (r w) -> p r w", w=94)[:, t * 8:(t + 1) * 8, 15 + s0:15 + s0 + 64]
                               for q in range(2)]
                        outv = [bk[q * 64:(q + 1) * 64, :].rearrange("p (r w) -> p r w", w=64) for q in range(2)]
                    else:
                        valid = [(2 * m + j <= 30) and not (t * 8 + 7 + s0 + j < 0 or t * 8 + s0 + j > 31) for j in range(2)]
                        if not any(valid):
                            continue
                        r0 = 15 + t * 8 + s0
                        if r0 < 0 or (r0 + 8) * 64 > LA:
                            assert False, (cb, t, m, r0)
                        rhs = [AS[b_][q][:, r0 * 64:(r0 + 8) * 64] for q in range(2)]
                        outv = [bk[q * 64:(q + 1) * 64, :] for q in range(2)]
                    groups_.append((m, rhs, outv))
                for i, (m, rhs, outv) in enumerate(groups_):
                    for q in range(2):
                        self.mm(outv[q], dg2[b_][:, q, m, :], rhs[q], i == 0, i == len(groups_) - 1, [bas, bdg], [bb])
                self.act(y[:, cb, t * 512:(t + 1) * 512], bk, AF.Identity, [bb, CONST[1]], [B_y[cb][t]], bias=vcol(VC_CB + cb))

        load_pair(0)
        conv_proj_stage(0)
        for cb in range(8):
            if cb + 1 < 8:
                if (cb + 1) % 2 == 0:
                    load_pair((cb + 1) // 2)
                conv_proj_stage(cb + 1)
            conv_stage(cb)
        self.prefetch(("z", 0), [(w_in, Z, 512)])
        CONV_BUFS = B_ap + B_AS + B_dg + B_sg

        if self.stop == "C1a":
            return True
        mean_b = self.view(R2, F32, (T,), name='mean_b')
        rstd_b = self.view(R2 + 8 * KB, F32, (T,), name='rstd_b')
        B_mean = [Buf("mean%d" % t) for t in range(4)]
        B_rstd = [Buf("rstd%d" % t) for t in range(4)]
        ysq = self.ysq_r
        B_ysq = [Buf("ysq%d" % i) for i in range(2)]
        msq = self.view(R3 + 4 * KB, F32, (512,))
        var = self.view(R3 + 6 * KB, F32, (512,))
        sd = self.view(R3 + 8 * KB, F32, (512,))
        B_msq, B_var, B_sd = Buf("msq"), Buf("var"), Buf("sd")
        self.inherit(B_mean + B_rstd + [B_msq, B_var, B_sd], CONV_BUFS)
        ones_f, ones_r = C["ones_f"], C["ones_r"]
        for t in range(4):
            bks, bbs = self.bank()
            for cb in range(8):
                self.mm(bks, ones_f, y[:, cb, t * 512:(t + 1) * 512], cb == 0, cb == 7, [B_c2, B_y[cb][t]], [bbs])
            bkq, bbq = self.bank()
            for cb in range(8):
                yq, byq = ysq[cb % 2], B_ysq[cb % 2]
                self.act(yq, y[:, cb, t * 512:(t + 1) * 512], AF.Square, [B_y[cb][t]], [byq])
                self.mm(bkq, ones_r, yq, cb == 0, cb == 7, [B_c2, byq], [bbq], sig=True)
            self.act(mean_b[:, t * 512:(t + 1) * 512], bks, AF.Copy, [bbs], [B_mean[t]], scale=1.0 / D)
            self.act(msq, bks, AF.Square, [bbs], [B_msq], scale=1.0 / D)
            self.stt(var, bkq, 1.0 / D, msq, ALU.mult, ALU.subtract, [bbq, B_msq], [B_var])
            self.act(sd, var, AF.Sqrt, [B_var], [B_sd], bias=EPS)
            self.recip(rstd_b[:, t * 512:(t + 1) * 512], sd, [B_sd], [B_rstd[t]])

        if self.stop == "C1b":
            return True
        t1 = [self.view(R3 + 10 * KB + i * 2 * KB, F32, (512,)) for i in range(2)]
        t2 = [self.view(R3 + 14 * KB + i * 2 * KB, F32, (512,)) for i in range(2)]
        szb = [self.view(R3 + 18 * KB + i * KB, BF16, (512,)) for i in range(2)]
        sb_ = [self.view(R3 + 20 * KB + i * KB, BF16, (512,)) for i in range(2)]
        B_t1 = [Buf("t1_%d" % i) for i in range(2)]
        B_t2 = [Buf("t2_%d" % i) for i in range(2)]
        B_sz = [Buf("sz%d" % i) for i in range(2)]
        B_s = [Buf("s%d" % i) for i in range(2)]
        self.inherit(B_t1 + B_t2 + B_sz + B_s, CONV_BUFS)
        czT = [self.view(RY + cb * 8 * KB, BF16, (T,), name='czT%d' % cb) for cb in range(8)]
        B_cz = [[Buf("cz%d_%d" % (cb, t)) for t in range(4)] for cb in range(8)]
        wz = {}
        wz[0] = self.wget(("z", 0), [(w_in, Z, 512)])
        self.prefetch(("cp", 0), [(self.conv_proj, 0, 512)])
        i = 0
        for cb in range(8):
            if cb == 2:
                wz[1] = self.wload([(w_in, Z + 512, 512)])
            if cb == 5:
                self.prefetch(("mgc", 0), [(w_in, MGC, 512)])
            wv, wb = wz[cb // 4]
            for t in range(4):
                bkz, bbz = self.bank()
                proj_fm(bkz, bbz, wv, wb, (cb % 4) * 128, 128, 256 + t * 512, 512, LAT[t])
                j = i % 2
                i += 1
                self.act(szb[j], bkz, AF.Silu, [bbz, CONST[1]], [B_sz[j]], bias=vcol(VC_BZ + cb))
                ysl = y[:, cb, t * 512:(t + 1) * 512]
                self.tt(t1[j], ysl, mean_b[:, t * 512:(t + 1) * 512], ALU.subtract, [B_y[cb][t], B_mean[t]], [B_t1[j]], eng="gpsimd")
                self.tt(t2[j], t1[j], rstd_b[:, t * 512:(t + 1) * 512], ALU.mult, [B_t1[j], B_rstd[t]], [B_t2[j]])
                self.act(sb_[j], t2[j], AF.Silu, [B_t2[j], CONST[1]], [B_s[j]], bias=vcol(VC_LNB + cb), scale=vcol(VC_LNG + cb))
                self.tt(czT[cb][:, t * 512:(t + 1) * 512], sb_[j], szb[j], ALU.mult, [B_s[j], B_sz[j]],
                        [B_cz[cb][t], B_y[cb][t // 2]])

        if self.stop == "C1c":
            return True
        m1T = self.view(R2, BF16, (8, T), name='m1T')
        B_m1 = [[Buf("m1_%d_%d" % (nbk, t)) for t in range(4)] for nbk in range(8)]
        self.inherit([b for row in B_m1 for b in row], B_mean + B_rstd)
        sgm = [self.view(R3 + i * 2 * KB, F32, (512,)) for i in range(2)]
        B_sgm = [Buf("sgm%d" % i) for i in range(2)]
        self.inherit(B_sgm, CONV_BUFS + [B_msq, B_var, B_sd])
        i = 0
        for half in range(2):
            wcp = self.wget(("cp", half), [(self.conv_proj, half * 512, 512)])
            wmg = self.wget(("mgc", half), [(w_in, MGC + half * 512, 512)])
            for q4 in range(4):
                nbk = half * 4 + q4
                for t in range(4):
                    bkc, bbc = self.bank()
                    for cb in range(8):
                        self.mm(bkc, wcp[0][:, cb, q4 * 128:(q4 + 1) * 128], czT[cb][:, t * 512:(t + 1) * 512], cb == 0, cb == 7,
                                [wcp[1], B_cz[cb][t]], [bbc])
                    bkm, bbm = self.bank()
                    proj_fm(bkm, bbm, wmg[0], wmg[1], q4 * 128, 128, 256 + t * 512, 512, LAT[t])
                    j = i % 2
                    i += 1
                    self.act(sgm[j], bkm, AF.Sigmoid, [bbm, CONST[1]], [B_sgm[j]], bias=vcol(VC_BMGC + nbk))
                    self.tt(m1T[:, nbk, t * 512:(t + 1) * 512], bkc, sgm[j], ALU.mult, [bbc, B_sgm[j]], [B_m1[nbk][t]])
        self.prefetch(("lr", 0), [(w_in, AFO, 32)])
        C1_BUFS = [b for row in B_cz for b in row] + B_sgm + B_t1 + B_t2 + B_sz + B_s + [B_msq, B_var, B_sd]

        if self.stop == "C1d":
            return True
        ogT = self.view(RY, BF16, (8, T), name='ogT')
        B_og = [[Buf("og%d_%d" % (blk, c)) for c in range(16)] for blk in range(8)]
        HB = RY + 32 * KB
        qT = self.view(HB, BF16, (T,))
        kT = self.view(HB + 4 * KB, BF16, (TT,))
        kv_tm = self.view(HB + 4 * KB + 4608, BF16, (NT, 384))
        lrT = [self.view(R3 + i * 4608, BF16, (TT,), parts=17, name='lrT%d' % i) for i in range(2)]
        o_f = self.view(R3 + 9216, F32, (16, 256), name='o_f')
        G0 = R3 + 9216 + 16 * KB
        goff = [G0]

        def galloc(nbytes):
            r = goff[0]
            goff[0] += (nbytes + 31) // 32 * 32
            return r

        e1 = self.view(galloc(2048), F32, (512,))
        sp = self.ysq_r
        ekl = self.view(galloc(1024), BF16, (512,))
        ebT = self.view(galloc(1024), BF16, (512,))
        enbT = self.view(galloc(1024), BF16, (512,))
        kl = [self.view(HB + 22528 + i * 1024, BF16, (512,)) for i in range(2)]
        qeT = [self.view(HB + 22528 + 2048 + i * 1024, BF16, (512,)) for i in range(2)]
        keT = [self.view(HB + 22528 + 4096 + i * 1024, BF16, (512,)) for i in range(2)]
        A_sb = [self.view(galloc(256), BF16, (128,)) for _ in range(2)]
        ebl = [self.view(galloc(32), F32, (4,)) for _ in range(2)]
        S_f = [self.view(galloc(1024), F32, (256,)) for _ in range(2)]
        S_bf = [self.view(galloc(512), BF16, (256,)) for _ in range(2)]
        ong = [self.view(HB + 28672 + i * 2048, BF16, (4, 256)) for i in range(2)]
        onj = self.view(galloc(512), BF16, (256,))
        fin = self.view(galloc(192), F32, (48,))
        assert goff[0] <= self.arena_bytes, (goff[0], self.arena_bytes)
        mk = lambda n: [Buf(n + str(i)) for i in range(2)]
        B_sp, B_kl, B_qeT, B_keT, B_A, B_ebl = (mk(n) for n in ["sp", "kl", "qeT", "keT", "A", "ebl"])
        B_e1, B_ekl, B_ebT, B_enbT = Buf("e1"), Buf("ekl"), Buf("ebT"), Buf("enbT")
        B_Sf = mk("Sf")
        B_Sbf, B_ong = mk("Sbf"), mk("ong")
        B_fin = Buf("fin")
        B_lr = [Buf("lr0"), Buf("lr1")]
        B_of = [Buf("of%d" % c) for c in range(16)]
        B_q = [Buf("q%d" % t) for t in range(4)]
        B_k = [Buf("k%d" % t) for t in range(5)]
        B_kv = [Buf("kv%d" % t) for t in range(NT)]
        self.inherit([b for row in B_og for b in row] + B_lr + B_of + B_q + B_k + B_kv + B_kl + B_qeT + B_keT + B_A + B_ebl + B_Sf + B_Sbf
                     + [B_e1, B_ekl, B_ebT, B_enbT, B_fin], C1_BUFS)
        tri_r, mask_f, mask_b = C["tri_r"], C["mask_f"], C["mask_b"]
        upb, kvb_b = C["upb"], C["kvb_b"]

        wlr = self.wget(("lr", 0), [(w_in, AFO, 32)])
        for d_ in range(2):
            self.memset(lrT[d_], 1.0, [B_lr[d_]], eng="vector")
        toktiles = [(0, 256, B_uT[0:2] + B_uTo[0:2])] + [(256 + t * 512, 512, LAT[t]) for t in range(4)]
        for d_ in range(2):
            for (tok0, ntok, rb) in toktiles:
                bk, bb = self.bank()
                proj_fm(bk, bb, wlr[0], wlr[1], d_ * 16, 16, tok0, ntok, rb)
                self.act(lrT[d_][0:16, tok0:tok0 + ntok], bk[0:16, 0:ntok], AF.Identity, [bb, CONST[1]], [B_lr[d_]],
                         bias=vecs[0:16, VC_BAF + d_:VC_BAF + d_ + 1])

        def load_head(h):
            w1 = self.wload([(w_in, Q + h * 128, 128), (w_in, K + h * 128, 128), (w_in, V + h * 256, 256)])
            return w1

        hw = {}

        def PROJ(h):
            (wv, wb) = hw.pop(h)
            for t in range(4):
                bk, bb = self.bank()
                proj_fm(bk, bb, wv, wb, 0, 128, 256 + t * 512, 512, LAT[t])
                self.ts(qT[:, t * 512:(t + 1) * 512], bk, vcol(VC_BQ + h), QSCALE, ALU.add, ALU.mult, [bb, CONST[1]], [B_q[t]])
            for ti, (tok0, ntok, rb) in enumerate(toktiles):
                bk, bb = self.bank()
                proj_fm(bk, bb, wv, wb, 128, 128, tok0, ntok, rb)
                self.act(kT[:, tok0:tok0 + ntok], bk[:, 0:ntok], AF.Identity, [bb, CONST[1]], [B_k[ti]], bias=vcol(VC_BK + h))
            for tile in range(NT):
                bk, bb = self.bank()
                for kb in range(8):
                    self.mm(bk[:, 0:384], uT[:, kb, tile * 128:(tile + 1) * 128], wv[:, kb, 128:512], kb == 0, kb == 7,
                            [wb, B_uT[tile], B_uTo[tile]], [bb])
                self.tt(kv_tm[:, tile, :], bk[:, 0:384], kvb_b[:, h * 384:(h + 1) * 384], ALU.add, [bb, CONST[2]], [B_kv[tile]])

        def SCAN(h):
            def kbuf(tile):
                return B_k[0] if tile < 2 else B_k[1 + (tile - 2) // 4]

            steps = []
            groups = []
            for d_ in range(2):
                if d_ == 0:
                    glist = [(0, 2, [0, 1])] + [(2 + 4 * g, 4, [2 + 4 * g + i for i in range(4)]) for g in range(4)]
                else:
                    glist = [(0, 2, [1, 0])] + [(2 + 4 * g, 4, [2 + 4 * g + i for i in (3, 2, 1, 0)]) for g in (3, 2, 1, 0)]
                n_in_dir = 0
                for (t0, n, tiles) in glist:
                    gi = len(groups)
                    groups.append((d_, t0, n))
                    for tile in tiles:
                        steps.append([d_, tile, gi, tile - t0, n_in_dir == 0, False])
                        n_in_dir += 1
                steps[-1][5] = True
            bku_of = {}
            bka_of = {}

            gst = {}

            def G_a(gi):
                d_, t0, n = groups[gi]
                g = gi % 2
                N = n * 128
                bkx, bbx = self.bank()
                for i in range(n):
                    tsl = slice((t0 + i) * 128, (t0 + i + 1) * 128)
                    self.mm(bkx[:, i * 128:(i + 1) * 128], lrT[d_][0:17, tsl], upb[d_][0:17, h * 128:(h + 1) * 128], True, True,
                            [B_lr[d_], CONST[3]], [bbx])
                self.act(e1[:, 0:N], bkx[:, 0:N], AF.Exp, [bbx], [B_e1], scale=-1.0)
                self.act(sp[g][:, 0:N], e1[:, 0:N], AF.Ln, [B_e1], [B_sp[g]], bias=1.0)

            def G_b(gi):
                d_, t0, n = groups[gi]
                g = gi % 2
                N = n * 128
                lat = t0 >= 2
                TRI = tri_r[0] if d_ == 0 else tri_r[1]
                KLM = tri_r[2] if d_ == 0 else tri_r[3]
                lastcol = 127 if d_ == 0 else 0
                bkk, bbk = self.bank()
                for i in range(n):
                    self.mm(bkk[:, i * 128:(i + 1) * 128], KLM, sp[g][:, i * 128:(i + 1) * 128], True, True, [B_c2, B_sp[g]], [bbk])
                self.act(ekl[:, 0:N], bkk[:, 0:N], AF.Exp, [bbk], [B_ekl])
                bkb, bbb = self.bank()
                for i in range(n):
                    self.mm(bkb[:, i * 128:(i + 1) * 128], sp[g][:, i * 128:(i + 1) * 128], TRI, True, True, [B_c2, B_sp[g]], [bbb])
                self.act(ebl[g][:, 0:n], bkb[:, 0:N].rearrange("p (n l) -> p n l", l=128)[:, :, lastcol], AF.Exp, [bbb], [B_ebl[g]])
                if lat:
                    self.act(ebT[:, 0:N], bkb[:, 0:N], AF.Exp, [bbb], [B_ebT])
                    self.act(enbT[:, 0:N], bkb[:, 0:N], AF.Exp, [bbb], [B_enbT], scale=-1.0)

            def G_c(gi):
                d_, t0, n = groups[gi]
                g = gi % 2
                N = n * 128
                lat = t0 >= 2
                self.tt(kl[g][:, 0:N].rearrange("p (n d) -> p n d", d=128), kv_tm[:, t0:t0 + n, 0:128],
                        ekl[:, 0:N].rearrange("p (n d) -> p n d", d=128), ALU.mult, B_kv[t0:t0 + n] + [B_ekl], [B_kl[g]])
                if lat:
                    c0 = t0 - 2
                    self.tt(qeT[g][:, 0:N], qT[:, c0 * 128:c0 * 128 + N], ebT[:, 0:N], ALU.mult, [B_q[c0 // 4], B_ebT], [B_qeT[g]])
                    self.tt(keT[g][:, 0:N], kT[:, t0 * 128:t0 * 128 + N], enbT[:, 0:N], ALU.mult, [kbuf(t0), B_enbT], [B_keT[g]])

            def ST1(si):
                d_, tile, gi, i, first, last = steps[si]
                g = gi % 2
                j = si % 2
                MASK = mask_f if d_ == 0 else mask_b
                isl = slice(i * 128, (i + 1) * 128)
                bku, bbu = self.bank(pin=True)
                self.mm(bku[:, 0:256], kl[g][:, isl], kv_tm[:, tile, 128:384], True, True, [B_kl[g], B_kv[tile]], [bbu])
                bku_of[si] = (bku, bbu)
                if tile >= 2:
                    bka, bba = self.bank()
                    self.mm(bka[:, 0:128], keT[g][:, isl], qeT[g][:, isl], True, True, [B_keT[g], B_qeT[g]], [bba])
                    self.tt(A_sb[j], bka[:, 0:128], MASK, ALU.mult, [bba, B_c2], [B_A[j]])

            sbi = [0]
            spp = [0]

            def ST2(si):
                d_, tile, gi, i, first, last = steps[si]
                g = gi % 2
                j = si % 2
                c = tile - 2
                isl = slice(i * 128, (i + 1) * 128)
                if first:
                    self.memset(S_f[spp[0]], 0.0, [B_Sf[spp[0]]], eng="vector")
                    self.memset(S_bf[sbi[0]], 0.0, [B_Sbf[sbi[0]]], eng="vector")
                bku, bbu = bku_of.pop(si)
                if tile >= 2:
                    bko, bbo = self.bank(pin=True)
                    self.mm(bko[:, 0:256], A_sb[j], kv_tm[:, tile, 128:384], True, False, [B_A[j], B_kv[tile]], [bbo])
                    self.mm(bko[:, 0:256], qeT[g][:, isl], S_bf[sbi[0]], False, True, [B_qeT[g], B_Sbf[sbi[0]]], [bbo])
                if not last:
                    nxt = sbi[0] ^ 1
                    p0 = spp[0]
                    p1 = p0 ^ 1
                    self.stt(S_f[p1], S_f[p0], ebl[g][:, i:i + 1], bku[:, 0:256], ALU.mult, ALU.add, [B_Sf[p0], B_ebl[g], bbu], [B_Sf[p1]])
                    self.cp(S_bf[nxt], S_f[p1], [B_Sf[p1]], [B_Sbf[nxt]], eng="scalar")
                    spp[0] = p1
                    sbi[0] = nxt
                self.unpin(bku)
                if len(pend_o) > 0:
                    OEVAC()
                if tile >= 2:
                    pend_o.append((d_, c, bko, bbo))

            pend_o = []

            def OEVAC():
                d_, c, bko, bbo = pend_o.pop(0)
                if d_ == 0:
                    self.cp(o_f[:, c, :], bko[:, 0:256], [bbo], [B_of[c]], eng="vector")
                else:
                    self.tt(o_f[:, c, :], bko[:, 0:256], o_f[:, c, :], ALU.add, [bbo, B_of[c]], [B_of[c]])
                self.unpin(bko)

            G_a(0)
            G_b(0)
            G_c(0)
            ST1(0)
            pos = 0
            for si in range(len(steps)):
                gi = steps[si][2]
                n_g = groups[gi][2]
                pos = 0 if (si == 0 or steps[si - 1][2] != gi) else pos + 1
                if gi + 1 < len(groups):
                    if pos == 0:
                        G_a(gi + 1)
                    if pos == min(1, n_g - 1):
                        G_b(gi + 1)
                    if pos == min(2, n_g - 1):
                        G_c(gi + 1)
                if si + 1 < len(steps):
                    ST1(si + 1)
                ST2(si)
            while pend_o:
                OEVAC()

        on_v = [self.view(R3 + 9216 + c * 1024, BF16, (256,)) for c in range(16)]

        def F_ACT(h):
            for c in range(16):
                self.act(onj, o_f[:, c, :], AF.Square, [B_of[c]], [B_fin], accum=fin[:, c:c + 1])
            self.act(fin[:, 16:32], fin[:, 0:16], AF.Ln, [B_fin], [B_fin], bias=EPS, scale=1.0 / 256)
            self.act(fin[:, 32:48], fin[:, 16:32], AF.Exp, [B_fin], [B_fin], scale=-0.5)
            for c in range(16):
                self.act(on_v[c], o_f[:, c, :], AF.Identity, [B_of[c], B_fin], [B_of[c]], scale=fin[:, 32 + c:33 + c])

        def F_PE(h):
            for gq in range(4):
                for e_ in range(2):
                    bkt, bbt = self.bank()
                    bktb = bkt.bitcast(BF16)
                    for i in range(4):
                        c = gq * 4 + i
                        self.tr(bktb[:, i * 128:(i + 1) * 128], on_v[c][:, e_ * 128:(e_ + 1) * 128], [B_of[c], B_c2], [bbt], signal=(i == 3))
                    blk = h * 2 + e_
                    self.ts(ogT[:, blk, gq * 512:(gq + 1) * 512], bktb[:, 0:512], vcol(VC_GNG + e_), None,
                            ALU.mult, ALU.bypass, [bbt, CONST[1]], B_og[blk][gq * 4:(gq + 1) * 4])

        hw[0] = load_head(0)
        PROJ(0)
        for h in range(4):
            if h < 3:
                hw[h + 1] = load_head(h + 1)
            SCAN(h)
            F_ACT(h)
            if h < 3:
                PROJ(h + 1)
            F_PE(h)
        self.prefetch(("r", 0), [(w_in, R, 512)])
        self.prefetch(("r", 1), [(w_in, R + 512, 512)])
        C2_BUFS = B_lr + B_of + B_q + B_k + B_kv + B_kl + B_qeT + B_keT + B_A + B_ebl + B_Sf + B_Sbf + [B_e1, B_ekl, B_ebT, B_enbT, B_fin]

        if self.stop == "C2":
            return True
        sgg = [self.view(R3 + i * 2 * KB, F32, (512,)) for i in range(2)]
        tmp = [self.view(R3 + 4 * KB + i * 2 * KB, F32, (512,)) for i in range(2)]
        B_sgg, B_tmp = mk("sgg"), mk("tmp")
        srt = [self.view(R3 + 8 * KB + i * KB, BF16, (512,)) for i in range(2)]
        B_srt = mk("srt")
        self.inherit(B_sgg + B_tmp + B_srt, C2_BUFS)
        i = 0
        for half in range(2):
            wr = self.wget(("r", half), [(w_in, R + half * 512, 512)])
            for q4 in range(4):
                blk = half * 4 + q4
                for t in range(4):
                    bk, bb = self.bank()
                    proj_fm(bk, bb, wr[0], wr[1], q4 * 128, 128, 256 + t * 512, 512, LAT[t])
                    j = i % 2
                    i += 1
                    self.act(srt[j], bk, AF.Silu, [bb, CONST[1]], [B_srt[j]], bias=vcol(VC_BR + blk))
                    osl = ogT[:, blk, t * 512:(t + 1) * 512]
                    self.tt(osl, osl, srt[j], ALU.mult, B_og[blk][t * 4:(t + 1) * 4] + [B_srt[j]], B_og[blk][t * 4:(t + 1) * 4])
        i = 0
        for half in range(2):
            wgp = self.wload([(self.gla_proj, half * 512, 512)])
            wmg = self.wload([(w_in, MGG + half * 512, 512)])
            for q4 in range(4):
                nbk = half * 4 + q4
                for t in range(4):
                    bkc, bbc = self.bank()
                    for blk in range(8):
                        self.mm(bkc, wgp[0][:, blk, q4 * 128:(q4 + 1) * 128], ogT[:, blk, t * 512:(t + 1) * 512], blk == 0, blk == 7,
                                [wgp[1]] + B_og[blk][t * 4:(t + 1) * 4], [bbc])
                    bkm, bbm = self.bank()
                    proj_fm(bkm, bbm, wmg[0], wmg[1], q4 * 128, 128, 256 + t * 512, 512, LAT[t])
                    j = i % 2
                    i += 1
                    self.act(sgg[j], bkm, AF.Sigmoid, [bbm, CONST[1]], [B_sgg[j]], bias=vcol(VC_BMGG + nbk))
                    self.tt(tmp[j], bkc, sgg[j], ALU.mult, [bbc, B_sgg[j]], [B_tmp[j]])
                    msl = m1T[:, nbk, t * 512:(t + 1) * 512]
                    self.tt(msl, tmp[j], msl, ALU.add, [B_tmp[j], B_m1[nbk][t]], [B_m1[nbk][t]])
        self.prefetch(("wo", 0), [(self.w_out, 0, 512)])
        C3_BUFS = [b for row in B_og for b in row] + B_sgg + B_tmp + B_srt

        if self.stop == "C3":
            return True
        NXS = 6
        xs = [self.view(RY + i * 4 * KB, F32, (1024,)) for i in range(3)] + \
             [self.view(RY + 42 * KB + i * 4 * KB, F32, (1024,)) for i in range(3)]
        if not hasattr(self, "d_x6"):
            self.d_x6 = C["d_x"] + [S.new_dsem() for _ in range(3)]
        hh = [self.view(RY + 12 * KB + i * 4 * KB, F32, (1024,)) for i in range(3)]
        ot = [self.view(RY + 24 * KB + i * 4 * KB, F32, (1024,)) for i in range(3)]
        junk = self.view(RY + 36 * KB, BF16, (1024,))
        B_xs = [Buf("xs%d" % i) for i in range(6)]
        B_hh = [Buf("hh%d" % i) for i in range(3)]
        B_ot = [Buf("ot%d" % i) for i in range(3)]
        B_fsL = [Buf("fs%d" % i) for i in range(16)]
        B_fng = Buf("fng")
        B_junk4 = Buf("junk4")
        self.inherit(B_xs + B_hh + B_ot + [B_fng, B_junk4], C3_BUFS + C2_BUFS)
        gate_b = C["gate_b"][bi]
        fng_b = self.view(RY + 38 * KB, F32, (1024,))
        S.dma("sync", lambda e: e.dma_start(out=fng_b, in_=self.fng_d), C["d_const"], writes=[B_fng])
        wo = [self.wget(("wo", half), [(self.w_out, half * 512, 512)]) for half in range(2)]
        d_out3 = C["d_out"] + [C["d_out2"]]

        def c4_A(c):
            x_t, bx = xs[c % NXS], B_xs[c % NXS]
            S.dma("sync", lambda e, x_t=x_t, c=c: e.dma_start(out=x_t, in_=self.xcat[bi, TC + c * 128:TC + (c + 1) * 128, :]),
                  self.d_x6[c % NXS], writes=[bx])
            j = c % 3
            for half in range(2):
                bk, bb = self.bank()
                for cb in range(8):
                    self.mm(bk, m1T[:, cb, c * 128:(c + 1) * 128], wo[half][0][:, cb, :], cb == 0, cb == 7,
                            [wo[half][1], B_m1[cb][c // 4]], [bb])
                hs = hh[j][:, half * 512:(half + 1) * 512]
                self.tt(hs, bk, gate_b[:, half * 512:(half + 1) * 512], ALU.mult, [bb, B_c2], [B_hh[j]])
            self.tt(hh[j], hh[j], x_t, ALU.add, [B_hh[j], bx], [B_hh[j]], eng="gpsimd")

        def c4_B(c):
            j = c % 3
            B_fs = B_fsL[c]
            fs = small[:, 96 + 4 * (c % 8):100 + 4 * (c % 8)]
            self.act(junk, hh[j], AF.Square, [B_hh[j]], [B_fs, B_junk4], accum=fs[:, 0:1])
            self.act(fs[:, 1:2], fs[:, 0:1], AF.Ln, [B_fs], [B_fs], bias=EPS, scale=1.0 / D)
            self.act(fs[:, 2:3], fs[:, 1:2], AF.Exp, [B_fs], [B_fs], scale=-0.5)

        def c4_C(c):
            j = c % 3
            B_fs = B_fsL[c]
            fs = small[:, 96 + 4 * (c % 8):100 + 4 * (c % 8)]
            self.stt(ot[j], hh[j], fs[:, 2:3], fng_b, ALU.mult, ALU.mult, [B_hh[j], B_fs, B_fng], [B_ot[j]])
            o_t = ot[j]
            S.dma("sync", lambda e, o_t=o_t, c=c: e.dma_start(out=self.out_d[bi, c * 128:(c + 1) * 128, :], in_=o_t),
                  d_out3[j], reads=[B_ot[j]])

        for i in range(16 + 2):
            if i < 16:
                c4_A(i)
            if 0 <= i - 1 < 16:
                c4_B(i - 1)
            if 0 <= i - 2 < 16:
                c4_C(i - 2)
        S.barrier()
        self.prev = None
        return self.stop == "C4"


def _prep_shared(inp):
    f = lambda a: np.ascontiguousarray(np.asarray(a, dtype=np.float32))
    b_in = f(inp["b_in"])[0]
    vec = np.zeros((128, NV), np.float32)

    def fm(v, nblk):
        return v.reshape(nblk, 128).T

    ada_b = f(inp["ada_b"])[0]
    vec[:, VC_ADAB:VC_ADAB + 16] = fm(ada_b[0:2048], 16)
    vec[:, VC_NG:VC_NG + 8] = fm(f(inp["norm_g"])[0], 8)
    vec[:, VC_BGV:VC_BGV + 8] = fm(b_in[GV:GV + 1024], 8)
    vec[:, VC_BGG:VC_BGG + 8] = fm(b_in[GG:GG + 1024], 8)
    vec[:, VC_BZ:VC_BZ + 8] = fm(b_in[Z:Z + 1024], 8)
    vec[:, VC_BQ:VC_BQ + 4] = fm(b_in[Q:Q + 512], 4)
    vec[:, VC_BK:VC_BK + 4] = fm(b_in[K:K + 512], 4)
    vec[:, VC_BR:VC_BR + 8] = fm(b_in[R:R + 1024], 8)
    vec[:, VC_BMGC:VC_BMGC + 8] = fm(b_in[MGC:MGC + 1024], 8)
    vec[:, VC_BMGG:VC_BMGG + 8] = fm(b_in[MGG:MGG + 1024], 8)
    vec[:, VC_CB:VC_CB + 8] = fm(f(inp["conv_b"])[0], 8)
    vec[:, VC_LNG:VC_LNG + 8] = fm(f(inp["conv_ln_g"])[0], 8)
    vec[:, VC_LNB:VC_LNB + 8] = fm(f(inp["conv_ln_b"])[0], 8)
    vec[:, VC_GNG:VC_GNG + 2] = fm(f(inp["gla_norm_g"])[0], 2)
    vec[0:16, VC_BAF] = b_in[AFO:AFO + 16]
    vec[0:16, VC_BAB] = b_in[ABO:ABO + 16]
    cw = f(inp["conv_w"])[0]
    cwp = np.concatenate([cw, np.zeros((1, 1024), np.float32)], axis=0)
    cw5 = cwp.reshape(16, 2, 8, 2, 64)
    vec[:, VC_CW:VC_CW + 256] = cw5.transpose(1, 4, 2, 3, 0).reshape(128, 256)
    kvrow = np.concatenate([np.concatenate([b_in[K + h * 128:K + (h + 1) * 128], b_in[V + h * 256:V + (h + 1) * 256]]) for h in range(4)])
    kvb = np.ascontiguousarray(np.broadcast_to(kvrow[None, :], (128, 1536)))
    adag = np.ascontiguousarray(np.broadcast_to(ada_b[None, 2048:3072], (128, 1024)))
    fng = np.ascontiguousarray(np.broadcast_to(f(inp["final_norm_g"])[None, :], (128, 1024)))
    idx = np.arange(128)
    le = (idx[:, None] <= idx[None, :]).astype(np.float32)
    ge = (idx[:, None] >= idx[None, :]).astype(np.float32)
    gt = (idx[:, None] > idx[None, :]).astype(np.float32)
    lt = (idx[:, None] < idx[None, :]).astype(np.float32)
    s = np.float32(-1.0 / 16.0)
    si = np.concatenate([np.eye(64, dtype=np.float32), np.eye(64, dtype=np.float32)], axis=0)
    consts = np.concatenate([np.eye(128, dtype=np.float32), le, ge, s * le, s * ge, s * gt, s * lt, si], axis=1)
    upb = np.stack([
        np.concatenate([f(inp["decay_up_fwd"])[0], f(inp["decay_bias_fwd"])[0][None, :]], axis=0),
        np.concatenate([f(inp["decay_up_bwd"])[0], f(inp["decay_bias_bwd"])[0][None, :]], axis=0)], axis=0)
    shared = dict(kvb=kvb, adag=adag, fng=fng, consts=np.ascontiguousarray(consts), upb=np.ascontiguousarray(upb),
                  ada_w=f(inp["ada_w"])[0], w_in=f(inp["w_in"])[0], conv_proj=f(inp["conv_proj"])[0],
                  gla_proj=f(inp["gla_proj"])[0], w_out=f(inp["w_out"])[0])
    return vec, shared


_NC_CACHE = {}


def _get_nc(nb):
    if nb not in _NC_CACHE:
        p = Prog(nb=nb)
        _NC_CACHE[nb] = p.build()
    return _NC_CACHE[nb]


def kernel(**inp):
    x = np.asarray(inp["x"], dtype=np.float32)
    ctx = np.asarray(inp["ctx"], dtype=np.float32)
    c = np.asarray(inp["c"], dtype=np.float32)
    c_ctx = np.asarray(inp["c_ctx"], dtype=np.float32)
    Bn = x.shape[0]
    ncores = 8
    nb = Bn // ncores
    vec0, shared = _prep_shared(inp)
    in_maps = []
    for core in range(ncores):
        b0 = core * nb
        xcat = np.ascontiguousarray(np.concatenate([ctx[b0:b0 + nb], x[b0:b0 + nb]], axis=1))
        vec = vec0.copy()
        cols = np.stack([c[b0 + i] if i < nb else c_ctx for i in range(3)] if nb == 2 else
                        [c[b0], c[b0], c_ctx], axis=1)
        vec[:, VC_C:VC_C + 24] = cols.reshape(8, 128, 3).transpose(1, 0, 2).reshape(128, 24)
        m = dict(shared)
        m["xcat"] = xcat
        m["vecs"] = vec
        in_maps.append(m)
    nc = _get_nc(nb)
    res = run_bass_kernel_spmd(nc, in_maps, core_ids=list(range(ncores)))
    out = np.concatenate([np.asarray(r["out"]) for r in res.results], axis=0)
    return out.astype(np.float32)
```
